# Optimizing a Trainium2 kernel written in Bass

```python
import math
import jax, jax.numpy as jnp
from jax import lax
import numpy as np

D_MODEL = 2048
BATCH = 4
SEQ = 2048
DEPTH = 4

CONV_WIDTH = D_MODEL // 2
ATTN_WIDTH = D_MODEL - CONV_WIDTH
DIFF_HEAD_DIM = 64
V_HEAD_DIM = 2 * DIFF_HEAD_DIM
N_DIFF_HEADS = ATTN_WIDTH // V_HEAD_DIM
CONV_KERNEL = 31
D_FF = -(-8 * D_MODEL // (3 * 256)) * 256
IN_WIDTH = 2 * CONV_WIDTH + 3 * ATTN_WIDTH
Q_BLOCK = 128
EPS = 1e-6
LN_EPS = 1e-5

kernel_name = "hymba_conformer_conv_diff_attn_swiglu"


def rmsnorm(x, g):
    xf = x.astype(jnp.float32)
    y = xf * lax.rsqrt(jnp.mean(xf * xf, axis=-1, keepdims=True) + EPS)
    return (y * g.astype(jnp.float32)).astype(x.dtype)


def conformer_conv(u, w_dw, b_dw, ln_g, ln_b):
    a, gate = jnp.split(u, 2, axis=-1)
    h = a * jax.nn.sigmoid(gate)
    h = jnp.pad(h, ((0, 0), (CONV_KERNEL - 1, 0), (0, 0)))
    h = lax.conv_general_dilated(
        h, w_dw[:, None, :], window_strides=(1,), padding='VALID',
        dimension_numbers=('NWC', 'WIO', 'NWC'),
        feature_group_count=CONV_WIDTH) + b_dw
    hf = h.astype(jnp.float32)
    mu = jnp.mean(hf, axis=-1, keepdims=True)
    var = jnp.mean(jnp.square(hf - mu), axis=-1, keepdims=True)
    hf = (hf - mu) * lax.rsqrt(var + LN_EPS) * ln_g.astype(jnp.float32) + ln_b.astype(jnp.float32)
    return jax.nn.silu(hf).astype(u.dtype)


def diff_attention(q, k, v, lam):
    seq = q.shape[1]
    scale = DIFF_HEAD_DIM ** -0.5
    outs = []
    for i in range(seq // Q_BLOCK):
        q0 = i * Q_BLOCK
        kend = q0 + Q_BLOCK
        qb = q[:, q0:kend]
        kb = k[:, :kend]
        vb = v[:, :kend]
        s = jnp.einsum('bqhcd,bkhcd->bhcqk', qb, kb).astype(jnp.float32) * scale
        qpos = q0 + jnp.arange(Q_BLOCK)[:, None]
        kpos = jnp.arange(kend)[None, :]
        s = jnp.where(kpos <= qpos, s, -jnp.inf)
        p = jax.nn.softmax(s, axis=-1)
        a = p[:, :, 0] - lam * p[:, :, 1]
        outs.append(jnp.einsum('bhqk,bkhd->bqhd', a.astype(v.dtype), vb))
    return jnp.concatenate(outs, axis=1)


def setup_inputs(seed: int = 0) -> dict:
    key = jax.random.key(seed)
    ks = jax.random.split(key, 20)
    f32 = jnp.float32
    nrm = lambda k, shape, s: jax.random.normal(k, shape, f32) * s
    return {
        "x": nrm(ks[0], (BATCH, SEQ, D_MODEL), 1.0),
        "norm1_g": 1.0 + nrm(ks[1], (DEPTH, D_MODEL), 0.02),
        "w_in": nrm(ks[2], (DEPTH, D_MODEL, IN_WIDTH), D_MODEL ** -0.5),
        "conv_w": nrm(ks[3], (DEPTH, CONV_KERNEL, CONV_WIDTH), CONV_KERNEL ** -0.5),
        "conv_b": nrm(ks[4], (DEPTH, CONV_WIDTH), 0.02),
        "conv_ln_g": 1.0 + nrm(ks[5], (DEPTH, CONV_WIDTH), 0.02),
        "conv_ln_b": nrm(ks[6], (DEPTH, CONV_WIDTH), 0.02),
        "lam_q1": nrm(ks[7], (DEPTH, DIFF_HEAD_DIM), 0.1),
        "lam_k1": nrm(ks[8], (DEPTH, DIFF_HEAD_DIM), 0.1),
        "lam_q2": nrm(ks[9], (DEPTH, DIFF_HEAD_DIM), 0.1),
        "lam_k2": nrm(ks[10], (DEPTH, DIFF_HEAD_DIM), 0.1),
        "subln_g": 1.0 + nrm(ks[11], (DEPTH, V_HEAD_DIM), 0.02),
        "w_out": nrm(ks[12], (DEPTH, D_MODEL, D_MODEL), D_MODEL ** -0.5),
        "norm2_g": 1.0 + nrm(ks[13], (DEPTH, D_MODEL), 0.02),
        "w_gate": nrm(ks[14], (DEPTH, D_MODEL, D_FF), D_MODEL ** -0.5),
        "w_up": nrm(ks[15], (DEPTH, D_MODEL, D_FF), D_MODEL ** -0.5),
        "w_down": nrm(ks[16], (DEPTH, D_FF, D_MODEL), D_FF ** -0.5),
        "final_g": 1.0 + nrm(ks[17], (D_MODEL,), 0.02),
    }


def reference(x, norm1_g, w_in, conv_w, conv_b, conv_ln_g, conv_ln_b,
              lam_q1, lam_k1, lam_q2, lam_k2, subln_g, w_out, norm2_g,
              w_gate, w_up, w_down, final_g):
    B, S, _ = x.shape
    c2 = 2 * CONV_WIDTH
    for l in range(DEPTH):
        lam_init = 0.8 - 0.6 * math.exp(-0.3 * l)
        h = rmsnorm(x, norm1_g[l])
        z = h @ w_in[l]
        u_conv = z[..., :c2]
        q = z[..., c2:c2 + ATTN_WIDTH].reshape(B, S, N_DIFF_HEADS, 2, DIFF_HEAD_DIM)
        k = z[..., c2 + ATTN_WIDTH:c2 + 2 * ATTN_WIDTH].reshape(B, S, N_DIFF_HEADS, 2, DIFF_HEAD_DIM)
        v = z[..., c2 + 2 * ATTN_WIDTH:].reshape(B, S, N_DIFF_HEADS, V_HEAD_DIM)

        conv_out = conformer_conv(u_conv, conv_w[l], conv_b[l], conv_ln_g[l], conv_ln_b[l])

        lam = (jnp.exp(jnp.sum(lam_q1[l].astype(jnp.float32) * lam_k1[l].astype(jnp.float32)))
               - jnp.exp(jnp.sum(lam_q2[l].astype(jnp.float32) * lam_k2[l].astype(jnp.float32)))
               + lam_init)
        o = diff_attention(q, k, v, lam)
        o = rmsnorm(o, subln_g[l]) * (1.0 - lam_init)
        attn_out = o.reshape(B, S, ATTN_WIDTH)

        mix = jnp.concatenate([conv_out, attn_out], axis=-1) @ w_out[l]
        x = x + mix
        h = rmsnorm(x, norm2_g[l])
        x = x + (jax.nn.silu(h @ w_gate[l]) * (h @ w_up[l])) @ w_down[l]
    return rmsnorm(x, final_g)
```

```python
import math
import types
from contextlib import ExitStack
import numpy as np
import ml_dtypes
import concourse.bass as bass
import concourse.mybir as mybir
from concourse.bass_utils import run_bass_kernel_spmd

F32 = mybir.dt.float32
BF16 = mybir.dt.bfloat16
ALU = mybir.AluOpType
AF = mybir.ActivationFunctionType
AX = mybir.AxisListType

D = 2048
S = 2048
DEPTH = 4
CW = 1024
NH = 8
KCV = 31
DFF = 5632
INW = 5120
TH = 1024
NT = 512
EPS = 1e-6
LN_EPS = 1e-5
NCORES = 4
ENG = ("pe", "act", "dve", "pool", "sp")


def _freeze(fn):
    if fn is None or fn.__closure__ is None:
        return fn
    cells = []
    for c in fn.__closure__:
        try:
            cells.append(types.CellType(c.cell_contents))
        except ValueError:
            cells.append(c)
    return types.FunctionType(fn.__code__, fn.__globals__, fn.__name__, fn.__defaults__, tuple(cells))


class Prog:
    def __init__(self):
        self.ops = {e: [] for e in ENG}
        self.cnt = {}
        self.epoch = 0
        self.last_write = {}
        self.readers = {}
        self.waited = {e: {} for e in ENG}
        self.dma_cnt = {}
        self.dma_rr = {}
        self.semkeys = set()

    def _deps(self, reads, writes):
        deps = set()
        for r in reads:
            lw = self.last_write.get(r)
            if lw:
                deps.add(lw)
        for w in writes:
            lw = self.last_write.get(w)
            if lw:
                deps.add(lw)
            deps |= self.readers.get(w, set())
        return deps

    def _waits(self, eng, deps):
        waits = []
        for (key, val) in sorted(deps, key=lambda t: (str(t[0]), t[1])):
            if eng == "pe" and key[0] == "pe":
                continue
            if self.waited[eng].get(key, 0) >= val:
                continue
            self.waited[eng][key] = val
            waits.append((key, val))
        return waits

    def _finish(self, tok, reads, writes):
        for w in writes:
            self.last_write[w] = tok
            self.readers[w] = set()
        for r in reads:
            self.readers.setdefault(r, set()).add(tok)

    def op(self, eng, fn, reads=(), writes=(), signal=True):
        waits = self._waits(eng, self._deps(reads, writes))
        key = (eng, self.epoch)
        self.semkeys.add(key)
        if signal:
            self.cnt[key] = self.cnt.get(key, 0) + 1
            tok = (key, self.cnt[key])
            inc = (key, 1)
        else:
            tok = (key, self.cnt.get(key, 0) + 1)
            inc = None
        self.ops[eng].append((_freeze(fn), waits, inc))
        self._finish(tok, reads, writes)
        return tok

    def dma(self, queue, stream, nsem, fn, reads=(), writes=()):
        idx = self.dma_rr.get(stream, 0)
        self.dma_rr[stream] = idx + 1
        key = ("dma", stream, idx % nsem)
        self.semkeys.add(key)
        deps = self._deps(reads, writes)
        prev = self.dma_cnt.get(key, 0)
        if prev:
            deps.add((key, prev))
        waits = self._waits(queue, deps)
        self.dma_cnt[key] = prev + 16
        tok = (key, prev + 16)
        self.ops[queue].append((_freeze(fn), waits, (key, 16)))
        self._finish(tok, reads, writes)
        return tok

    def wait_all(self, eng, toks):
        waits = self._waits(eng, set(toks))
        self.ops[eng].append((None, waits, None))


def build(n_layers=DEPTH, debug=False):
    nc = bass.Bass("TRN2", target_bir_lowering=False)
    P = Prog()

    def din(name, shape, dt=F32):
        return nc.dram_tensor(name, list(shape), dt, kind="ExternalInput")

    x_in = din("x", [S, D])
    norm1_g = din("norm1_g", [DEPTH, D])
    w_in = din("w_in", [DEPTH, D, INW])
    conv_w = din("conv_w", [DEPTH, KCV, CW])
    conv_b = din("conv_b", [DEPTH, CW])
    conv_ln_g = din("conv_ln_g", [DEPTH, CW])
    conv_ln_b = din("conv_ln_b", [DEPTH, CW])
    lam_d = [din(n, [DEPTH, 64]) for n in ("lam_q1", "lam_k1", "lam_q2", "lam_k2")]
    subln_g = din("subln_g", [DEPTH, 128])
    w_out = din("w_out", [DEPTH, D, D])
    norm2_g = din("norm2_g", [DEPTH, D])
    w_gate = din("w_gate", [DEPTH, D, DFF])
    w_up = din("w_up", [DEPTH, D, DFF])
    w_down = din("w_down", [DEPTH, DFF, D])
    final_g = din("final_g", [D])
    ident_d = din("c_ident", [128, 128])
    mask_d = din("c_mask", [128, 128], BF16)
    y_out = nc.dram_tensor("y", [S, D], F32, kind="ExternalOutput")

    skind = "ExternalOutput" if debug else "Internal"
    xT_d = nc.dram_tensor("xT_d", [128, 16, S], F32, kind=skind)
    kT_d = nc.dram_tensor("kT_d", [NH, 128, S], BF16, kind=skind)
    V_d = nc.dram_tensor("V_d", [NH, 128, 16, 128], BF16, kind=skind)
    c32_d = nc.dram_tensor("c32_d", [128, 8, TH], F32, kind=skind)
    if debug:
        dbg_mix = nc.dram_tensor("dbg_mix", [2, 128, 16, TH], BF16, kind="ExternalOutput")
        dbg_q = nc.dram_tensor("dbg_q", [2, 128, 8 * TH], BF16, kind="ExternalOutput")
        dbg_x0 = nc.dram_tensor("dbg_x0", [128, 16, S], F32, kind="ExternalOutput")

    def sb(name, shape, dt):
        return nc.alloc_sbuf_tensor(name, list(shape), dt)

    hm = sb("hm", [128, 16, TH], BF16)
    big32 = sb("big32", [128, 11264], F32)
    bigb = big32[:, :].bitcast(BF16)
    NW = 4
    wb = [sb(f"wb{i}", [128, 4096], BF16) for i in range(NW)]
    kh = [sb(f"kh{i}", [128, S], BF16) for i in range(2)]
    vh = [sb(f"vh{i}", [128, 16, 132], BF16) for i in range(2)]
    NXB = 3
    xbuf = [sb(f"xb{i}", [128, TH], F32) for i in range(NXB)]
    sqb = [sb(f"sq{i}", [128, NT], F32) for i in range(2)]
    glu = [sb(f"glu{i}", [128, 30 + TH], BF16) for i in range(2)]
    dg = sb("dg", [128, KCV, 128], BF16)
    cacc = [sb(f"cacc{i}", [128, TH], F32) for i in range(2)]
    lnmu = sb("lnmu", [128, TH], F32)
    lnrs = sb("lnrs", [128, TH], F32)
    nrstd = sb("nrstd", [128, TH], F32)
    sig2 = [sb(f"sig{i}", [128, TH], F32) for i in range(2)]
    sig = sig2[0]
    t1b = sig2[1]
    NPT = 4
    ptb = [sb(f"pt{i}", [128, NT], BF16) for i in range(NPT)]
    o1b = [sb(f"o1b{i}", [128, 128], F32) for i in range(2)]
    o2b = [sb(f"o2b{i}", [128, 128], F32) for i in range(4)]
    onb = [sb(f"onb{i}", [128, 128], F32) for i in range(4)]
    sm = [sb(f"sm{i}", [128, 8], F32) for i in range(2)]
    kst = [sb(f"kst{i}", [128, NT], BF16) for i in range(2)]
    vst = [sb(f"vst{i}", [128, 256], BF16) for i in range(2)]
    halo = sb("halo", [128, 8, 30], BF16)
    ident = sb("ident", [128, 128], F32)
    identb = sb("identb", [128, 128], BF16)
    maskT = sb("maskT", [128, 128], BF16)
    ones = sb("ones", [128, 128], F32)
    epsn = sb("epsn", [128, 1], F32)
    epsl = sb("epsl", [128, 1], F32)
    prms = [sb(f"prm{i}", [128, 128], F32) for i in range(4)]
    g1c = sb("g1c", [128, 64], F32)
    g2c = sb("g2c", [128, 64], F32)
    gfc = sb("gfc", [128, 16], F32)
    cbc = sb("cbc", [128, 32], F32)
    lgc = sb("lgc", [128, 32], F32)
    lbc = sb("lbc", [128, 32], F32)
    sgc = sb("sgc", [128, 4], F32)
    cwc = sb("cwc", [128, DEPTH * 8 * KCV], F32)
    lamrow = xbuf[0]
    lamb = sig2[0]
    osb = [[sb(f"osb{c}{q}", [128, 132], F32) for q in range(4)] for c in range(2)]
    lame = sb("lame", [128, 16], F32)
    neglam = sb("neglam", [128, 4], F32)

    ps = [nc.alloc_psum_tensor(f"ps{i}", [128, NT], F32) for i in range(8)]

    def bank(i):
        return ("bank", i)

    P.dma("sp", "misc", 2, lambda e: e.dma_start(out=ident[:, :], in_=ident_d.ap()), writes=["ident"])
    P.dma("sp", "misc", 2, lambda e: e.dma_start(out=maskT[:, :], in_=mask_d.ap()), writes=["maskT"])
    P.op("dve", lambda e: e.memset(ones[:, :], 1.0), writes=["ones"])
    P.op("dve", lambda e: e.memset(epsn[:, :], EPS), writes=["epsn"])
    P.op("dve", lambda e: e.memset(epsl[:, :], LN_EPS), writes=["epsl"])
    P.op("dve", lambda e: e.tensor_copy(out=identb[:, :], in_=ident[:, :]), reads=["ident"], writes=["identb"])
    for i in range(2):
        P.op("dve", lambda e, i=i: e.memset(vh[i][:, :, 128:129], 1.0), writes=[("vh1", i)])

    trs = {"n": 0}

    def tr_load(src_ap, r, dst_ap, tag):
        i = trs["n"]
        trs["n"] += 1
        prm = prms[i % 4]
        pb = 4 + i % 4
        P.dma("sp", "misc", 4, lambda e: e.dma_start(out=prm[0:r, :], in_=src_ap), writes=[("prm", i % 4)])
        P.op("pe", lambda e: e.transpose(ps[pb][:, 0:r], prm[0:r, :], ident[0:r, 0:r]),
             reads=[("prm", i % 4), "ident"], writes=[bank(pb)])
        P.op("dve", lambda e: e.tensor_copy(out=dst_ap, in_=ps[pb][:, 0:r]), reads=[bank(pb)], writes=[tag])

    tr_load(norm1_g.ap().rearrange("l (c p) -> (l c) p", p=128), 64, g1c[:, :], "g1c")
    tr_load(norm2_g.ap().rearrange("l (c p) -> (l c) p", p=128), 64, g2c[:, :], "g2c")
    tr_load(final_g.ap().rearrange("(c p) -> c p", p=128), 16, gfc[:, :], "gfc")
    tr_load(conv_b.ap().rearrange("l (c p) -> (l c) p", p=128), 32, cbc[:, :], "cbc")
    tr_load(conv_ln_g.ap().rearrange("l (c p) -> (l c) p", p=128), 32, lgc[:, :], "lgc")
    tr_load(conv_ln_b.ap().rearrange("l (c p) -> (l c) p", p=128), 32, lbc[:, :], "lbc")
    tr_load(subln_g.ap(), 4, sgc[:, :], "sgc")
    for l in range(n_layers):
        lam_init = 0.8 - 0.6 * math.exp(-0.3 * l)
        P.op("dve", lambda e, l=l, v=1.0 - lam_init: e.tensor_scalar(
            sgc[:, l:l + 1], sgc[:, l:l + 1], v, None, ALU.mult), reads=["sgc"], writes=["sgc"])
        for c in range(8):
            o = (l * 8 + c) * KCV
            tr_load(conv_w.ap()[l, :, c * 128:(c + 1) * 128], KCV, cwc[:, o:o + KCV], "cwc")
    for j in range(4):
        P.dma("sp", "misc", 2, lambda e, j=j: e.dma_start(
            out=lamrow[0:1, j * 256:(j + 1) * 256],
            in_=lam_d[j].ap().rearrange("l d -> (l d)").unsqueeze(0)), writes=["lamrow"])
    for j in range(2):
        P.op("pe", lambda e, j=j: e.matmul(ps[6][:, :], lhsT=ones[0:1, :], rhs=lamrow[0:1, j * 512:(j + 1) * 512],
                                           start=True, stop=True),
             reads=["ones", "lamrow"], writes=[bank(6)])
        P.op("dve", lambda e, j=j: e.tensor_copy(out=lamb[:, j * 512:(j + 1) * 512], in_=ps[6][:, :]),
             reads=[bank(6)], writes=["lamb"])
    for j in range(2):
        P.op("dve", lambda e, j=j: e.tensor_tensor(
            out=lamb[:, j * 512:j * 512 + 256], in0=lamb[:, j * 512:j * 512 + 256],
            in1=lamb[:, j * 512 + 256:j * 512 + 512], op=ALU.mult), reads=["lamb"], writes=["lamb"])
        P.op("dve", lambda e, j=j: e.reduce_sum(
            out=lame[:, j * 4:(j + 1) * 4],
            in_=lamb[:, j * 512:j * 512 + 256].rearrange("p (l d) -> p l d", d=64), axis=AX.X),
            reads=["lamb"], writes=["lame"])
    P.op("act", lambda e: e.activation(out=lame[:, 8:16], in_=lame[:, 0:8], func=AF.Exp),
         reads=["lame"], writes=["lame"])
    P.op("dve", lambda e: e.tensor_tensor(out=neglam[:, :], in0=lame[:, 12:16], in1=lame[:, 8:12], op=ALU.subtract),
         reads=["lame"], writes=["neglam"])
    for l in range(n_layers):
        lam_init = 0.8 - 0.6 * math.exp(-0.3 * l)
        P.op("dve", lambda e, l=l, v=-lam_init: e.tensor_scalar(
            neglam[:, l:l + 1], neglam[:, l:l + 1], v, None, ALU.add), reads=["neglam"], writes=["neglam"])

    xrow = [big32[:, 0:2048], big32[:, 2048:4096]]
    stg = [big32[:, 4096:6144], big32[:, 6144:8192]]
    for tb in range(16):
        sl = tb % 2
        P.dma("sp", "xl", 3, lambda e, tb=tb, sl=sl: e.dma_start(
            out=xrow[sl], in_=x_in.ap()[tb * 128:(tb + 1) * 128, :]), writes=[("xrow", sl)])
        for g4 in range(4):
            b = (tb * 4 + g4) % 4
            for j in range(4):
                kc = g4 * 4 + j
                P.op("pe", lambda e, b=b, j=j, kc=kc, sl=sl: e.transpose(
                    ps[b][:, j * 128:(j + 1) * 128], xrow[sl][:, kc * 128:(kc + 1) * 128], ident[:, :]),
                    reads=[("xrow", sl), "ident"], writes=[bank(b)], signal=(j == 3))
            eng = "act" if g4 % 2 else "dve"
            if eng == "act":
                P.op("act", lambda e, b=b, g4=g4, sl=sl: e.copy(out=stg[sl][:, g4 * 512:(g4 + 1) * 512], in_=ps[b][:, :]),
                     reads=[bank(b)], writes=[("stg", sl, g4)])
            else:
                P.op("dve", lambda e, b=b, g4=g4, sl=sl: e.tensor_copy(out=stg[sl][:, g4 * 512:(g4 + 1) * 512], in_=ps[b][:, :]),
                     reads=[bank(b)], writes=[("stg", sl, g4)])
        P.dma("sp", "xs", 3, lambda e, tb=tb, sl=sl: e.dma_start(
            out=xT_d[:, :, tb * 128:(tb + 1) * 128], in_=stg[sl].rearrange("p (k t) -> p k t", t=128)),
            reads=[("stg", sl, g) for g in range(4)], writes=[("xtb", tb)])

    if debug:
        for kc in range(16):
            tk = P.dma("sp", "dbg", 2, lambda e, kc=kc: e.dma_start(out=dbg_x0.ap()[:, kc, :], in_=xT_d[:, kc, :]),
                  reads=[("xtb", tb) for tb in range(16)], writes=[("dbgx0", kc)])
        P.wait_all("sp", [tk])
    wstate = {"n": 0}

    def wload(src3, ncols_total):
        i = wstate["n"]
        wstate["n"] += 1
        sl = i % NW
        kcn = src3.shape[1]
        ncol = src3.shape[2]
        dst = wb[sl][:, 0:kcn * ncol].rearrange("p (k n) -> p k n", n=ncol)
        P.dma("pool", "w", NW, lambda e: e.dma_start(out=dst, in_=src3), writes=[("wb", sl)])
        return sl, dst

    def xkeys(s, kc):
        return [("x", kc, s)] + [("xtb", tb) for tb in range(16)]

    xb_state = {"n": 0}

    def xload(s, kc):
        i = xb_state["n"]
        xb_state["n"] += 1
        sl = i % NXB
        P.dma("sp", "xl", 3, lambda e: e.dma_start(out=xbuf[sl][:, :], in_=xT_d[:, kc, s * TH:(s + 1) * TH]),
              reads=xkeys(s, kc), writes=[("xb", sl)])
        return sl

    def norm_stats(s):
        pend = [xload(s, 0), xload(s, 1)]
        for kc in range(16):
            sl = pend.pop(0)
            if kc + 2 < 16:
                pend.append(xload(s, kc + 2))
            for tt in range(2):
                q = (kc * 2 + tt) % 2
                P.op("act", lambda e, sl=sl, tt=tt, q=q: e.activation(
                    out=sqb[q][:, :], in_=xbuf[sl][:, tt * NT:(tt + 1) * NT], func=AF.Square),
                    reads=[("xb", sl)], writes=[("sq", q)])
                P.op("pe", lambda e, tt=tt, q=q, kc=kc: e.matmul(
                    ps[4 + tt][:, :], lhsT=ones[:, :], rhs=sqb[q][:, :], start=(kc == 0), stop=(kc == 15)),
                    reads=["ones", ("sq", q)], writes=[bank(4 + tt)], signal=True)
        for tt in range(2):
            P.op("act", lambda e, tt=tt: e.activation(
                out=nrstd[:, tt * NT:(tt + 1) * NT], in_=ps[4 + tt][:, :], func=AF.Sqrt,
                bias=epsn[:, 0:1], scale=1.0 / D), reads=[bank(4 + tt), "epsn"], writes=[("nrstd", tt)])
            P.op("dve", lambda e, tt=tt: e.reciprocal(
                out=nrstd[:, tt * NT:(tt + 1) * NT], in_=nrstd[:, tt * NT:(tt + 1) * NT]),
                reads=[("nrstd", tt)], writes=[("nrstd", tt)])

    def norm_apply_h(s, gcols):
        pend = [xload(s, 0), xload(s, 1)]
        for kc in range(16):
            sl = pend.pop(0)
            if kc + 2 < 16:
                pend.append(xload(s, kc + 2))
            P.op("dve", lambda e, sl=sl, kc=kc: e.scalar_tensor_tensor(
                out=hm[:, kc, :], in0=xbuf[sl][:, :], scalar=gcols[:, kc:kc + 1], in1=nrstd[:, :],
                op0=ALU.mult, op1=ALU.mult),
                reads=[("xb", sl), ("nrstd", 0), ("nrstd", 1), "g1c", "g2c"], writes=[("hm", kc)])

    def wsrc(wt, l, kc0, kcn, c0, ncol):
        return wt.ap()[l].rearrange("(k p) n -> p k n", p=128)[:, kc0:kc0 + kcn, c0:c0 + ncol]

    for l in range(n_layers):
        P.epoch = l
        lam_init = 0.8 - 0.6 * math.exp(-0.3 * l)
        for s in range(2):
            t0 = s * TH
            qT = lambda h: bigb[:, h * TH:(h + 1) * TH]
            qAB = lambda c, h: bigb[:, (c * 8 + h) * TH:(c * 8 + h + 1) * TH]
            qkeys = [("q", h, tt) for h in range(NH) for tt in range(2)] + [("act", fc) for fc in range(16)]
            if l == 0 and s == 0:
                qkeys = qkeys + [("xrow", 0), ("xrow", 1)] + [("stg", sl_, g_) for sl_ in range(2) for g_ in range(4)]
            P.op("pool", lambda e: e.memset(bigb[64:128, 0:8 * TH], 0.0), writes=qkeys)
            P.op("pool", lambda e: e.memset(bigb[0:64, 8 * TH:16 * TH], 0.0), writes=qkeys)
            actT = lambda fc: bigb[:, fc * TH:(fc + 1) * TH]
            norm_stats(s)
            norm_apply_h(s, g1c[:, l * 16:(l + 1) * 16])

            tiles = []
            for j in range(4):
                tiles.append(("k", j, 3072 + j * 256))
            for j in range(4):
                tiles.append(("v", j, 4096 + j * 256))
            for j in range(4):
                tiles.append(("q", j, 2048 + j * 256))
            for j in range(4):
                tiles.append(("g", j, 1024 + j * 256))
                tiles.append(("a", j, j * 256))
            loads = []

            def ensure_loads(upto, tiles=tiles, loads=loads, l=l):
                while len(loads) < min(upto, len(tiles)):
                    kind, j, c0 = tiles[len(loads)]
                    loads.append(wload(wsrc(w_in, l, 0, 16, c0, 256), 256))

            ensure_loads(3)
            gst = {"gi": 0}
            pending_conv = []
            if s == 0:
                for i in range(2):
                    P.op("dve", lambda e, i=i: e.memset(glu[i][:, 0:30], 0.0), writes=[("glu", i)])
            for ti, (kind, j, c0) in enumerate(tiles):
                ensure_loads(ti + 3)
                wsl, wv = loads[ti]
                if kind == "v":
                    for tb in range(8):
                        b = gst["gi"] % 4
                        gst["gi"] += 1
                        for kc in range(16):
                            P.op("pe", lambda e, b=b, kc=kc, tb=tb, wv=wv: e.matmul(
                                ps[b][:, 0:256], lhsT=hm[:, kc, tb * 128:(tb + 1) * 128], rhs=wv[:, kc, :],
                                start=(kc == 0), stop=(kc == 15)),
                                reads=[("hm", kc), ("wb", wsl)], writes=[bank(b)], signal=(kc == 15))
                        q = tb % 2
                        P.op("act", lambda e, b=b, q=q: e.copy(out=vst[q][:, :], in_=ps[b][:, 0:256]),
                             reads=[bank(b)], writes=[("vst", q)])
                        kb = s * 8 + tb
                        P.dma("sp", "kv", 4, lambda e, q=q, kb=kb, j=j: e.dma_start(
                            out=V_d.ap()[2 * j:2 * j + 2, :, kb, :].rearrange("h p d -> p h d"),
                            in_=vst[q][:, :].rearrange("p (h d) -> p h d", d=128)),
                            reads=[("vst", q)], writes=[("V", 2 * j, kb), ("V", 2 * j + 1, kb)])
                    continue
                for oc in range(2):
                    b0 = (gst["gi"] % 2) * 2
                    gst["gi"] += 1
                    for kc in range(16):
                        for tt in range(2):
                            P.op("pe", lambda e, b=b0 + tt, kc=kc, tt=tt, oc=oc, wv=wv: e.matmul(
                                ps[b][:, :], lhsT=wv[:, kc, oc * 128:(oc + 1) * 128],
                                rhs=hm[:, kc, tt * NT:(tt + 1) * NT], start=(kc == 0), stop=(kc == 15)),
                                reads=[("hm", kc), ("wb", wsl)], writes=[bank(b0 + tt)], signal=(kc == 15))
                    if pending_conv:
                        pending_conv.pop(0)()
                    h = 2 * j + oc
                    for tt in range(2):
                        b = b0 + tt
                        if kind == "k":
                            q = (h * 2 + tt) % 2
                            P.op("act", lambda e, b=b, q=q: e.copy(out=kst[q][:, :], in_=ps[b][:, :]),
                                 reads=[bank(b)], writes=[("kst", q)])
                            P.dma("sp", "kv", 4, lambda e, q=q, h=h, tt=tt: e.dma_start(
                                out=kT_d[h, :, t0 + tt * NT:t0 + (tt + 1) * NT], in_=kst[q][:, :]),
                                reads=[("kst", q)], writes=[("K", h, s * 2 + tt)])
                        elif kind == "q":
                            P.op("dve", lambda e, b=b, h=h, tt=tt: e.tensor_copy(
                                out=qAB(0, h)[0:64, tt * NT:(tt + 1) * NT], in_=ps[b][0:64, :]),
                                reads=[bank(b)], writes=[("q", h, tt)])
                            P.op("dve", lambda e, b=b, h=h, tt=tt: e.tensor_copy(
                                out=qAB(1, h)[64:128, tt * NT:(tt + 1) * NT], in_=ps[b][64:128, :]),
                                reads=[bank(b)], writes=[("q", h, tt)])
                        elif kind == "g":
                            P.op("act", lambda e, b=b, tt=tt, oc=oc: e.activation(
                                out=sig2[oc][:, tt * NT:(tt + 1) * NT], in_=ps[b][:, :], func=AF.Sigmoid),
                                reads=[bank(b)], writes=[("sig", oc, tt)])
                        else:
                            gs = h % 2
                            P.op("dve", lambda e, b=b, tt=tt, gs=gs: e.tensor_tensor(
                                out=glu[gs][:, 30 + tt * NT:30 + (tt + 1) * NT], in0=ps[b][:, :],
                                in1=sig2[gs][:, tt * NT:(tt + 1) * NT], op=ALU.mult),
                                reads=[bank(b), ("sig", gs, tt)], writes=[("glu", gs)])
                    if kind == "a":
                        c = h
                        gs = c % 2
                        if s == 1:
                            P.op("dve", lambda e, gs=gs, c=c: e.tensor_copy(out=glu[gs][:, 0:30], in_=halo[:, c, :]),
                                 reads=[("halo", c)], writes=[("glu", gs)])
                        else:
                            P.op("dve", lambda e, gs=gs, c=c: e.tensor_copy(out=halo[:, c, :], in_=glu[gs][:, TH:TH + 30]),
                                 reads=[("glu", gs)], writes=[("halo", c)])
                        wo = (l * 8 + c) * KCV
                        col = l * 8 + c
                        for k in range(KCV):
                            P.op("dve", lambda e, k=k, wo=wo: e.tensor_scalar(
                                dg[:, k, :], identb[:, :], cwc[:, wo + k:wo + k + 1], None, ALU.mult),
                                reads=["identb", "cwc"], writes=[("dg", k)])

                        def conv_task(c=c, gs=gs, col=col):
                            b0 = (gst["gi"] % 2) * 2
                            gst["gi"] += 1
                            for tt in range(2):
                                for k in range(KCV):
                                    P.op("pe", lambda e, b=b0 + tt, k=k, tt=tt: e.matmul(
                                        ps[b][:, :], lhsT=dg[:, k, :], rhs=glu[gs][:, k + tt * NT:k + (tt + 1) * NT],
                                        start=(k == 0), stop=(k == KCV - 1)),
                                        reads=[("dg", k), ("glu", gs)], writes=[bank(b0 + tt)], signal=(k == KCV - 1))
                            for tt in range(2):
                                q = tt
                                P.op("act", lambda e, b=b0 + tt, tt=tt: e.activation(
                                    out=cacc[gs][:, tt * NT:(tt + 1) * NT], in_=ps[b][:, :], func=AF.Identity,
                                    bias=cbc[:, col:col + 1], scale=1.0),
                                    reads=[bank(b0 + tt), "cbc"], writes=[("cacc", gs)])
                                P.op("act", lambda e, tt=tt, q=q: e.activation(
                                    out=sqb[q][:, :], in_=cacc[gs][:, tt * NT:(tt + 1) * NT], func=AF.Square),
                                    reads=[("cacc", gs)], writes=[("sq", q)])
                                P.op("pe", lambda e, tt=tt: e.matmul(
                                    ps[4 + tt][:, :], lhsT=ones[:, :], rhs=cacc[gs][:, tt * NT:(tt + 1) * NT],
                                    start=(c == 0), stop=(c == 7)),
                                    reads=["ones", ("cacc", gs)], writes=[bank(4 + tt)], signal=True)
                                P.op("pe", lambda e, tt=tt, q=q: e.matmul(
                                    ps[6 + tt][:, :], lhsT=ones[:, :], rhs=sqb[q][:, :],
                                    start=(c == 0), stop=(c == 7)),
                                    reads=["ones", ("sq", q)], writes=[bank(6 + tt)], signal=True)
                            P.dma("sp", "cs", 2, lambda e: e.dma_start(out=c32_d[:, c, :], in_=cacc[gs][:, :]),
                                  reads=[("cacc", gs)], writes=[("c32", c)])

                        pending_conv.append(conv_task)
            while pending_conv:
                pending_conv.pop(0)()
            for tt in range(2):
                tsl = slice(tt * NT, (tt + 1) * NT)
                P.op("dve", lambda e, tt=tt, tsl=tsl: e.tensor_scalar(
                    lnmu[:, tsl], ps[4 + tt][:, :], 1.0 / CW, None, ALU.mult),
                    reads=[bank(4 + tt)], writes=[("lnmu", tt)])
                P.op("dve", lambda e, tt=tt, tsl=tsl: e.tensor_tensor(
                    out=lnrs[:, tsl], in0=lnmu[:, tsl], in1=lnmu[:, tsl], op=ALU.mult),
                    reads=[("lnmu", tt)], writes=[("lnrs", tt)])
                P.op("dve", lambda e, tt=tt, tsl=tsl: e.scalar_tensor_tensor(
                    out=lnrs[:, tsl], in0=ps[6 + tt][:, :], scalar=1.0 / CW, in1=lnrs[:, tsl],
                    op0=ALU.mult, op1=ALU.subtract),
                    reads=[bank(6 + tt), ("lnrs", tt)], writes=[("lnrs", tt)])
                P.op("act", lambda e, tt=tt, tsl=tsl: e.activation(
                    out=lnrs[:, tsl], in_=lnrs[:, tsl], func=AF.Sqrt, bias=epsl[:, 0:1], scale=1.0),
                    reads=[("lnrs", tt), "epsl"], writes=[("lnrs", tt)])
                P.op("dve", lambda e, tt=tt, tsl=tsl: e.reciprocal(out=lnrs[:, tsl], in_=lnrs[:, tsl]),
                     reads=[("lnrs", tt)], writes=[("lnrs", tt)])
            pend = []

            def cload(c):
                i = xb_state["n"]
                xb_state["n"] += 1
                sl = i % NXB
                P.dma("sp", "xl", 3, lambda e: e.dma_start(out=xbuf[sl][:, :], in_=c32_d[:, c, :]),
                      reads=[("c32", c)], writes=[("xb", sl)])
                return sl

            pend = [cload(0), cload(1)]
            for c in range(8):
                sl = pend.pop(0)
                if c + 2 < 8:
                    pend.append(cload(c + 2))
                P.op("dve", lambda e, sl=sl: e.tensor_tensor(out=t1b[:, :], in0=xbuf[sl][:, :], in1=lnmu[:, :], op=ALU.subtract),
                     reads=[("xb", sl), ("lnmu", 0), ("lnmu", 1)], writes=[("sig", 1, 0), ("sig", 1, 1)])
                P.op("dve", lambda e: e.tensor_tensor(out=t1b[:, :], in0=t1b[:, :], in1=lnrs[:, :], op=ALU.mult),
                     reads=[("sig", 1, 0), ("sig", 1, 1), ("lnrs", 0), ("lnrs", 1)], writes=[("sig", 1, 0), ("sig", 1, 1)])
                col = l * 8 + c
                P.op("act", lambda e, c=c, col=col: e.activation(
                    out=hm[:, c, :], in_=t1b[:, :], func=AF.Silu, bias=lbc[:, col:col + 1], scale=lgc[:, col:col + 1]),
                    reads=[("sig", 1, 0), ("sig", 1, 1), "lbc", "lgc"], writes=[("hm", c)])

            nkb_all = (s + 1) * 8

            def kvload(h, sl):
                P.dma("sp", "kvl", 4, lambda e: e.dma_start(out=kh[sl][:, 0:nkb_all * 128], in_=kT_d[h, :, 0:nkb_all * 128]),
                      reads=[("K", h, i) for i in range((s + 1) * 2)], writes=[("kh", sl)])
                P.dma("sp", "kvl", 4, lambda e: e.dma_start(out=vh[sl][:, 0:nkb_all, 0:128], in_=V_d.ap()[h, :, 0:nkb_all, :]),
                      reads=[("V", h, kb) for kb in range(nkb_all)], writes=[("vh", sl)])

            steps = []
            for h in range(NH):
                for tt in range(2):
                    for c in range(2):
                        for kb in range(s * 8 + tt * 4 + 4):
                            steps.append((h, tt, c, kb))
            ep_pending = []

            def emit_qk(i):
                h, tt, c, kb = steps[i]
                hs = h % 2
                qb0 = s * 8 + tt * 4
                jd = kb - qb0
                n0 = max(jd, 0) * 128
                n = NT - n0
                sb_ = (0, 1, 7)[i % 3]
                pt = i % NPT
                diag = jd >= 0
                P.op("pe", lambda e: e.matmul(
                    ps[sb_][:, 0:n], lhsT=kh[hs][:, kb * 128:(kb + 1) * 128],
                    rhs=qAB(c, h)[:, tt * NT + n0:(tt + 1) * NT], start=True, stop=(not diag)),
                    reads=[("kh", hs), ("q", h, tt)], writes=[bank(sb_)], signal=(not diag))
                if diag:
                    P.op("pe", lambda e: e.matmul(
                        ps[sb_][:, 0:128], lhsT=identb[:, :], rhs=maskT[:, :], start=False, stop=True),
                        reads=["identb", "maskT"], writes=[bank(sb_)], signal=True)
                P.op("act", lambda e: e.activation(
                    out=ptb[pt][:, 0:n], in_=ps[sb_][:, 0:n], func=AF.Exp, scale=0.125),
                    reads=[bank(sb_)], writes=[("pt", pt)])

            def emit_av(i):
                h, tt, c, kb = steps[i]
                hs = h % 2
                qb0 = s * 8 + tt * 4
                n0 = max(kb - qb0, 0) * 128
                pt = i % NPT
                for qi in range(4):
                    if qi * 128 < n0:
                        continue
                    last = (kb == qb0 + qi)
                    P.op("pe", lambda e, qi=qi, last=last: e.matmul(
                        ps[2 + qi][:, 0:129], lhsT=ptb[pt][:, qi * 128 - n0:(qi + 1) * 128 - n0],
                        rhs=vh[hs][:, kb, 0:129], start=(kb == 0), stop=last),
                        reads=[("pt", pt), ("vh", hs), ("vh1", hs)], writes=[bank(2 + qi)], signal=last)
                if kb != qb0 + 3:
                    return
                for qi in range(4):
                    ob = 2 + qi
                    P.op("dve", lambda e, qi=qi, ob=ob: e.tensor_copy(out=osb[c][qi][:, 0:129], in_=ps[ob][:, 0:129]),
                         reads=[bank(ob)], writes=[("osb", c, qi)])
                for qi in range(4):
                    if c == 0:
                        break
                    P.op("dve", lambda e, qi=qi: e.reciprocal(out=sm[0][:, qi:qi + 1], in_=osb[0][qi][:, 128:129]),
                         reads=[("osb", 0, qi)], writes=[("sm0", qi)])
                    P.op("dve", lambda e, qi=qi: e.reciprocal(out=sm[1][:, qi:qi + 1], in_=osb[1][qi][:, 128:129]),
                         reads=[("osb", 1, qi)], writes=[("sm1", qi)])
                    P.op("dve", lambda e, qi=qi: e.tensor_scalar(
                        sm[1][:, qi:qi + 1], sm[1][:, qi:qi + 1], neglam[:, l:l + 1], None, ALU.mult),
                        reads=[("sm1", qi), "neglam"], writes=[("sm1", qi)])
                    P.op("dve", lambda e, qi=qi: e.tensor_scalar(
                        osb[0][qi][:, 0:128], osb[0][qi][:, 0:128], sm[0][:, qi:qi + 1], None, ALU.mult),
                        reads=[("osb", 0, qi), ("sm0", qi)], writes=[("osb", 0, qi)])
                    P.op("dve", lambda e, qi=qi: e.scalar_tensor_tensor(
                        out=o2b[qi][:, :], in0=osb[1][qi][:, 0:128], scalar=sm[1][:, qi:qi + 1],
                        in1=osb[0][qi][:, 0:128], op0=ALU.mult, op1=ALU.add),
                        reads=[("osb", 1, qi), ("sm1", qi), ("osb", 0, qi)], writes=[("o2", qi)])
                for qi in range(4):
                    if c == 0:
                        break
                    sl = qi % 2
                    P.op("dve", lambda e, sl=sl, qi=qi: e.tensor_tensor(
                        out=o1b[sl][:, :], in0=o2b[qi][:, :], in1=o2b[qi][:, :], op=ALU.mult),
                        reads=[("o2", qi)], writes=[("o1b", sl)])
                    P.op("dve", lambda e, sl=sl, qi=qi: e.reduce_sum(
                        out=sm[1][:, 4 + qi:5 + qi], in_=o1b[sl][:, :], axis=AX.X),
                        reads=[("o1b", sl)], writes=[("ss", qi)])
                    P.op("act", lambda e, qi=qi: e.activation(
                        out=sm[1][:, 4 + qi:5 + qi], in_=sm[1][:, 4 + qi:5 + qi], func=AF.Ln,
                        bias=epsn[:, 0:1], scale=1.0 / 128), reads=[("ss", qi), "epsn"], writes=[("ss", qi)])
                    P.op("act", lambda e, qi=qi: e.activation(
                        out=sm[1][:, 4 + qi:5 + qi], in_=sm[1][:, 4 + qi:5 + qi], func=AF.Exp, scale=-0.5),
                        reads=[("ss", qi)], writes=[("ss", qi)])
                    P.op("dve", lambda e, qi=qi: e.tensor_scalar(
                        onb[qi][:, :], o2b[qi][:, :], sm[1][:, 4 + qi:5 + qi], None, ALU.mult),
                        reads=[("o2", qi), ("ss", qi)], writes=[("on", qi)])
                if c == 1:
                    def part2(h=h, tt=tt):
                        for qi in range(4):
                            tb_ = 6
                            P.op("pe", lambda e, qi=qi, tb_=tb_: e.transpose(ps[tb_][:, 0:128], onb[qi][:, :], ident[:, :]),
                                 reads=[("on", qi), "ident"], writes=[bank(tb_)])
                            P.op("dve", lambda e, qi=qi, tb_=tb_: e.tensor_scalar(
                                hm[:, 8 + h, tt * NT + qi * 128:tt * NT + (qi + 1) * 128], ps[tb_][:, 0:128],
                                sgc[:, l:l + 1], None, ALU.mult),
                                reads=[bank(tb_), "sgc"], writes=[("hm", 8 + h)])
                    ep_pending.append((i + 6, part2))

            kvload(0, 0)
            LA = 3
            for i0 in range(LA):
                emit_qk(i0)
            for i in range(len(steps)):
                h, tt, c, kb = steps[i]
                if tt == 0 and c == 0 and kb == 0 and h + 1 < NH:
                    kvload(h + 1, (h + 1) % 2)
                emit_av(i)
                while ep_pending and ep_pending[0][0] <= i:
                    ep_pending.pop(0)[1]()
                if i + LA < len(steps):
                    emit_qk(i + LA)
            while ep_pending:
                ep_pending.pop(0)[1]()

            if debug and l == 0:
                out_dbg = []
                out_dbg.append(P.dma("sp", "dbg", 2, lambda e, s=s: e.dma_start(out=dbg_mix.ap()[s], in_=hm[:, :, :]),
                      reads=[("hm", c) for c in range(16)], writes=[("dbgmix", s)]))
                out_dbg.append(P.dma("sp", "dbg", 2, lambda e, s=s: e.dma_start(out=dbg_q.ap()[s], in_=bigb[:, 0:8 * TH]),
                      reads=[("q", h, tt) for h in range(8) for tt in range(2)], writes=[("dbgq", s)]))
                P.wait_all("sp", out_dbg)
            tiles = [(j, j * 256) for j in range(8)]
            loads = []

            def ensure_loads_o(upto, tiles=tiles, loads=loads, l=l):
                while len(loads) < min(upto, len(tiles)):
                    j, c0 = tiles[len(loads)]
                    loads.append(wload(wsrc(w_out, l, 0, 16, c0, 256), 256))

            def resid_gemm(ntiles, ensure, loads, kcn, rhs_fn, rhs_keys, gi0=0):
                gi = gi0
                for ti in range(ntiles):
                    ensure(ti + 3)
                    wsl, wv = loads[ti]
                    noc = wv.shape[2] // 128
                    for oc in range(noc):
                        ocg = ti * noc + oc
                        xsl = xload(s, ocg)
                        b0 = (gi % 2) * 2
                        gi += 1
                        for kc in range(kcn):
                            for tt in range(2):
                                P.op("pe", lambda e, b=b0 + tt, kc=kc, tt=tt, oc=oc, wv=wv: e.matmul(
                                    ps[b][:, :], lhsT=wv[:, kc, oc * 128:(oc + 1) * 128],
                                    rhs=rhs_fn(kc)[:, tt * NT:(tt + 1) * NT], start=(kc == 0), stop=(kc == kcn - 1)),
                                    reads=[rhs_keys(kc), ("wb", wsl)], writes=[bank(b0 + tt)], signal=(kc == kcn - 1))
                        for tt in range(2):
                            P.op("dve", lambda e, b=b0 + tt, tt=tt, xsl=xsl: e.tensor_tensor(
                                out=xbuf[xsl][:, tt * NT:(tt + 1) * NT], in0=xbuf[xsl][:, tt * NT:(tt + 1) * NT],
                                in1=ps[b][:, :], op=ALU.add), reads=[bank(b0 + tt), ("xb", xsl)], writes=[("xb", xsl)])
                        P.dma("sp", "xs", 3, lambda e, xsl=xsl, ocg=ocg: e.dma_start(
                            out=xT_d[:, ocg, s * TH:(s + 1) * TH], in_=xbuf[xsl][:, :]),
                            reads=[("xb", xsl)], writes=[("x", ocg, s)])

            ensure_loads_o(3)
            resid_gemm(8, ensure_loads_o, loads, 16, lambda kc: hm[:, kc, :], lambda kc: ("hm", kc))

            norm_stats(s)
            norm_apply_h(s, g2c[:, l * 16:(l + 1) * 16])

            for fh in range(2):
                f0 = fh * 2816
                tiles = []
                for j in range(11):
                    tiles.append(("g", j))
                    tiles.append(("u", j))
                loads = []

                def ensure_loads_f(upto, tiles=tiles, loads=loads, l=l, f0=f0):
                    while len(loads) < min(upto, len(tiles)):
                        kind, j = tiles[len(loads)]
                        wt = w_gate if kind == "g" else w_up
                        loads.append(wload(wsrc(wt, l, 0, 16, f0 + j * 256, 256), 256))

                ensure_loads_f(3)
                for j in range(11):
                    ensure_loads_f(2 * j + 4)
                    gsl, gv = loads[2 * j]
                    usl, uv = loads[2 * j + 1]
                    for oc in range(2):
                        fc = 2 * j + oc
                        bb = (fc % 2) * 4
                        for (wsl, wv, bo) in ((gsl, gv, 0), (usl, uv, 2)):
                            for kc in range(16):
                                for tt in range(2):
                                    P.op("pe", lambda e, b=bb + bo + tt, kc=kc, tt=tt, oc=oc, wv=wv: e.matmul(
                                        ps[b][:, :], lhsT=wv[:, kc, oc * 128:(oc + 1) * 128],
                                        rhs=hm[:, kc, tt * NT:(tt + 1) * NT], start=(kc == 0), stop=(kc == 15)),
                                        reads=[("hm", kc), ("wb", wsl)], writes=[bank(bb + bo + tt)], signal=(kc == 15))
                        for tt in range(2):
                            P.op("act", lambda e, b=bb + tt, tt=tt: e.activation(
                                out=sig[:, tt * NT:(tt + 1) * NT], in_=ps[b][:, :], func=AF.Silu),
                                reads=[bank(bb + tt)], writes=[("sig", 0, tt)])
                            P.op("dve", lambda e, b=bb + 2 + tt, tt=tt, fc=fc: e.tensor_tensor(
                                out=actT(fc)[:, tt * NT:(tt + 1) * NT], in0=ps[b][:, :],
                                in1=sig[:, tt * NT:(tt + 1) * NT], op=ALU.mult),
                                reads=[bank(bb + 2 + tt), ("sig", 0, tt)], writes=[("act", fc)])
                tiles = list(range(16))
                loads = []

                def ensure_loads_d(upto, tiles=tiles, loads=loads, l=l, fh=fh):
                    while len(loads) < min(upto, len(tiles)):
                        oc = tiles[len(loads)]
                        loads.append(wload(wsrc(w_down, l, fh * 22, 22, oc * 128, 128), 128))

                ensure_loads_d(3)
                resid_gemm(16, ensure_loads_d, loads, 22, lambda kc: actT(kc), lambda kc: ("act", kc))

    P.epoch = n_layers - 1
    rowst = big32[:, 0:8192]
    out_toks = []
    for s in range(2):
        norm_stats(s)
        for tt in range(2):
            pend = []

            def fload(kc, s=s, tt=tt):
                i = xb_state["n"]
                xb_state["n"] += 1
                sl = i % NXB
                P.dma("sp", "xl", 3, lambda e: e.dma_start(
                    out=xbuf[sl][:, 0:NT], in_=xT_d[:, kc, s * TH + tt * NT:s * TH + (tt + 1) * NT]),
                    reads=xkeys(s, kc), writes=[("xb", sl)])
                return sl

            pend = [fload(0), fload(1)]
            for kc in range(16):
                sl = pend.pop(0)
                if kc + 2 < 16:
                    pend.append(fload(kc + 2))
                q = kc % 2
                P.op("dve", lambda e, sl=sl, kc=kc, q=q, tt=tt: e.scalar_tensor_tensor(
                    out=sqb[q][:, :], in0=xbuf[sl][:, 0:NT], scalar=gfc[:, kc:kc + 1],
                    in1=nrstd[:, tt * NT:(tt + 1) * NT], op0=ALU.mult, op1=ALU.mult),
                    reads=[("xb", sl), ("nrstd", tt), "gfc"], writes=[("sq", q)])
                b = kc % 4
                for j in range(4):
                    P.op("pe", lambda e, b=b, j=j, q=q: e.transpose(
                        ps[b][:, j * 128:(j + 1) * 128], sqb[q][:, j * 128:(j + 1) * 128], ident[:, :]),
                        reads=[("sq", q), "ident"], writes=[bank(b)], signal=(j == 3))
                if kc % 2:
                    P.op("act", lambda e, b=b, kc=kc: e.copy(
                        out=rowst.rearrange("p (j f) -> p j f", f=D)[:, :, kc * 128:(kc + 1) * 128],
                        in_=ps[b][:, :].rearrange("p (j f) -> p j f", f=128)),
                        reads=[bank(b)], writes=[("rowst", kc)])
                else:
                    P.op("dve", lambda e, b=b, kc=kc: e.tensor_copy(
                        out=rowst.rearrange("p (j f) -> p j f", f=D)[:, :, kc * 128:(kc + 1) * 128],
                        in_=ps[b][:, :].rearrange("p (j f) -> p j f", f=128)),
                        reads=[bank(b)], writes=[("rowst", kc)])
            tok0 = s * TH + tt * NT
            out_toks.append(P.dma("sp", "ys", 2, lambda e, tok0=tok0: e.dma_start(
                out=y_out.ap()[tok0:tok0 + NT, :].rearrange("(j p) f -> p j f", p=128),
                in_=rowst.rearrange("p (j f) -> p j f", f=D)),
                reads=[("rowst", kc) for kc in range(16)], writes=[("y", s, tt)]))
    P.wait_all("sp", out_toks)

    nc.allow_low_precision("bf16 matmul operands with fp32 PSUM accumulation")
    with ExitStack() as es:
        sems = {}
        for i, key in enumerate(sorted(P.semkeys, key=str)):
            sems[key] = es.enter_context(nc.semaphore(f"s{i}"))
        block = es.enter_context(nc.Block())

        def replay(eng_name):
            def body(e):
                for (fn, waits, inc) in P.ops[eng_name]:
                    for (key, val) in waits:
                        e.wait_ge(sems[key], val)
                    if fn is None:
                        continue
                    ins = fn(e)
                    if inc is not None:
                        ins.then_inc(sems[inc[0]], inc[1])
            return body

        block.tensor(replay("pe"))
        block.scalar(replay("act"))
        block.vector(replay("dve"))
        block.gpsimd(replay("pool"))
        block.sync(replay("sp"))
    return nc


_CACHE = {}


def _consts():
    ident = np.eye(128, dtype=np.float32)
    k = np.arange(128)[:, None]
    q = np.arange(128)[None, :]
    mask = np.where(q >= k, 0.0, -30000.0).astype(np.float32).astype(ml_dtypes.bfloat16)
    return ident, mask


def kernel(**inputs):
    if "nc" not in _CACHE:
        _CACHE["nc"] = build(DEPTH)
    nc = _CACHE["nc"]
    ident, mask = _consts()
    shared = {}
    for k, v in inputs.items():
        if k == "x":
            continue
        shared[k] = np.ascontiguousarray(np.asarray(v, dtype=np.float32))
    shared["c_ident"] = ident
    shared["c_mask"] = mask
    x = np.asarray(inputs["x"], dtype=np.float32)
    in_maps = []
    for b in range(NCORES):
        m = dict(shared)
        m["x"] = np.ascontiguousarray(x[b])
        in_maps.append(m)
    res = run_bass_kernel_spmd(nc, in_maps, core_ids=list(range(NCORES)))
    return np.stack([np.asarray(res.results[b]["y"], dtype=np.float32) for b in range(NCORES)], axis=0)
```

```python
import math
import types
from contextlib import ExitStack
import numpy as np
import ml_dtypes
import concourse.bass as bass
import concourse.mybir as mybir
from concourse.bass_utils import run_bass_kernel_spmd

F32 = mybir.dt.float32
BF16 = mybir.dt.bfloat16
ALU = mybir.AluOpType
AF = mybir.ActivationFunctionType
AX = mybir.AxisListType

D = 2048
S = 2048
DEPTH = 4
CW = 1024
NH = 8
KCV = 31
DFF = 5632
INW = 5120
TH = 1024
NT = 512
EPS = 1e-6
LN_EPS = 1e-5
NCORES = 4
ENG = ("pe", "act", "dve", "pool", "sp")


def _freeze(fn):
    if fn is None or fn.__closure__ is None:
        return fn
    cells = []
    for c in fn.__closure__:
        try:
            cells.append(types.CellType(c.cell_contents))
        except ValueError:
            cells.append(c)
    return types.FunctionType(fn.__code__, fn.__globals__, fn.__name__, fn.__defaults__, tuple(cells))


class Prog:
    def __init__(self):
        self.ops = {e: [] for e in ENG}
        self.cnt = {}
        self.epoch = 0
        self.last_write = {}
        self.readers = {}
        self.waited = {e: {} for e in ENG}
        self.dma_cnt = {}
        self.dma_rr = {}
        self.semkeys = set()

    def _deps(self, reads, writes):
        deps = set()
        for r in reads:
            lw = self.last_write.get(r)
            if lw:
                deps.add(lw)
        for w in writes:
            lw = self.last_write.get(w)
            if lw:
                deps.add(lw)
            deps |= self.readers.get(w, set())
        return deps

    def _waits(self, eng, deps):
        waits = []
        for (key, val) in sorted(deps, key=lambda t: (str(t[0]), t[1])):
            if eng == "pe" and key[0] == "pe":
                continue
            if self.waited[eng].get(key, 0) >= val:
                continue
            self.waited[eng][key] = val
            waits.append((key, val))
        return waits

    def _finish(self, tok, reads, writes):
        for w in writes:
            self.last_write[w] = tok
            self.readers[w] = set()
        for r in reads:
            self.readers.setdefault(r, set()).add(tok)

    def op(self, eng, fn, reads=(), writes=(), signal=True):
        waits = self._waits(eng, self._deps(reads, writes))
        key = (eng, self.epoch)
        self.semkeys.add(key)
        if signal:
            self.cnt[key] = self.cnt.get(key, 0) + 1
            tok = (key, self.cnt[key])
            inc = (key, 1)
        else:
            tok = (key, self.cnt.get(key, 0) + 1)
            inc = None
        self.ops[eng].append((_freeze(fn), waits, inc))
        self._finish(tok, reads, writes)
        return tok

    def dma(self, queue, stream, nsem, fn, reads=(), writes=()):
        idx = self.dma_rr.get(stream, 0)
        self.dma_rr[stream] = idx + 1
        key = ("dma", stream, idx % nsem)
        self.semkeys.add(key)
        deps = self._deps(reads, writes)
        prev = self.dma_cnt.get(key, 0)
        if prev:
            deps.add((key, prev))
        waits = self._waits(queue, deps)
        self.dma_cnt[key] = prev + 16
        tok = (key, prev + 16)
        self.ops[queue].append((_freeze(fn), waits, (key, 16)))
        self._finish(tok, reads, writes)
        return tok

    def wait_all(self, eng, toks):
        waits = self._waits(eng, set(toks))
        self.ops[eng].append((None, waits, None))


def build(n_layers=DEPTH, debug=False):
    nc = bass.Bass("TRN2", target_bir_lowering=False)
    P = Prog()

    def din(name, shape, dt=F32):
        return nc.dram_tensor(name, list(shape), dt, kind="ExternalInput")

    x_in = din("x", [S, D])
    norm1_g = din("norm1_g", [DEPTH, D])
    w_in = din("w_in", [DEPTH, D, INW])
    conv_w = din("conv_w", [DEPTH, KCV, CW])
    conv_b = din("conv_b", [DEPTH, CW])
    conv_ln_g = din("conv_ln_g", [DEPTH, CW])
    conv_ln_b = din("conv_ln_b", [DEPTH, CW])
    lam_d = [din(n, [DEPTH, 64]) for n in ("lam_q1", "lam_k1", "lam_q2", "lam_k2")]
    subln_g = din("subln_g", [DEPTH, 128])
    w_out = din("w_out", [DEPTH, D, D])
    norm2_g = din("norm2_g", [DEPTH, D])
    w_gate = din("w_gate", [DEPTH, D, DFF])
    w_up = din("w_up", [DEPTH, D, DFF])
    w_down = din("w_down", [DEPTH, DFF, D])
    final_g = din("final_g", [D])
    ident_d = din("c_ident", [128, 128])
    mask_d = din("c_mask", [128, 128], BF16)
    y_out = nc.dram_tensor("y", [S, D], F32, kind="ExternalOutput")

    skind = "ExternalOutput" if debug else "Internal"
    xT_d = nc.dram_tensor("xT_d", [128, 16, S], F32, kind=skind)
    kT_d = nc.dram_tensor("kT_d", [NH, 128, S], BF16, kind=skind)
    V_d = nc.dram_tensor("V_d", [NH, 128, 16, 128], BF16, kind=skind)
    c32_d = nc.dram_tensor("c32_d", [128, 8, TH], F32, kind=skind)
    if debug:
        dbg_mix = nc.dram_tensor("dbg_mix", [2, 128, 16, TH], BF16, kind="ExternalOutput")
        dbg_q = nc.dram_tensor("dbg_q", [2, 128, 8 * TH], BF16, kind="ExternalOutput")
        dbg_x0 = nc.dram_tensor("dbg_x0", [128, 16, S], F32, kind="ExternalOutput")

    def sb(name, shape, dt):
        return nc.alloc_sbuf_tensor(name, list(shape), dt)

    hm = sb("hm", [128, 16, TH], BF16)
    big32 = sb("big32", [128, 11264], F32)
    bigb = big32[:, :].bitcast(BF16)
    NW = 4
    wb = [sb(f"wb{i}", [128, 4096], BF16) for i in range(NW)]
    kh = [sb(f"kh{i}", [128, S], BF16) for i in range(2)]
    vh = [sb(f"vh{i}", [128, 16, 132], BF16) for i in range(2)]
    NXB = 3
    xbuf = [sb(f"xb{i}", [128, TH], F32) for i in range(NXB)]
    sqb = [sb(f"sq{i}", [128, NT], F32) for i in range(2)]
    glu = [sb(f"glu{i}", [128, 30 + TH], BF16) for i in range(2)]
    dg = sb("dg", [128, KCV, 128], BF16)
    cacc = [sb(f"cacc{i}", [128, TH], F32) for i in range(2)]
    lnmu = sb("lnmu", [128, TH], F32)
    lnrs = sb("lnrs", [128, TH], F32)
    nrstd = sb("nrstd", [128, TH], F32)
    sig2 = [sb(f"sig{i}", [128, TH], F32) for i in range(2)]
    sig = sig2[0]
    t1b = sig2[1]
    NPT = 4
    ptb = [sb(f"pt{i}", [128, NT], BF16) for i in range(NPT)]
    sm = [sb(f"sm{i}", [128, 8], F32) for i in range(2)]
    kst = [sb(f"kst{i}", [128, NT], BF16) for i in range(2)]
    vst = [sb(f"vst{i}", [128, 256], BF16) for i in range(2)]
    halo = sb("halo", [128, 8, 30], BF16)
    ident = sb("ident", [128, 128], F32)
    identb = sb("identb", [128, 128], BF16)
    maskT = sb("maskT", [128, 128], BF16)
    ones = sb("ones", [128, 128], F32)
    epsn = sb("epsn", [128, 1], F32)
    epsl = sb("epsl", [128, 1], F32)
    prms = [sb(f"prm{i}", [128, 128], F32) for i in range(2)]
    sqbb = [sb(f"sqbb{i}", [128, NT], BF16) for i in range(2)]
    onesb = sb("onesb", [128, 128], BF16)
    g1c = sb("g1c", [128, 64], F32)
    g2c = sb("g2c", [128, 64], F32)
    gfc = sb("gfc", [128, 16], F32)
    cbc = sb("cbc", [128, 32], F32)
    lgc = sb("lgc", [128, 32], F32)
    lbc = sb("lbc", [128, 32], F32)
    sgc = sb("sgc", [128, 4], F32)
    cwc = sb("cwc", [128, DEPTH * 8 * KCV], F32)
    lamrow = xbuf[0]
    lamb = sig2[0]
    osb = [sb(f"osb{c}", [128, 4, 132], F32) for c in range(2)]
    o4 = sb("o4", [128, 4, 128], F32)
    sq4 = sb("sq4", [128, 4, 128], F32)
    on4 = sb("on4", [128, 4, 128], F32)
    lame = sb("lame", [128, 16], F32)
    neglam = sb("neglam", [128, 4], F32)

    ps = [nc.alloc_psum_tensor(f"ps{i}", [128, NT], F32) for i in range(8)]

    def bank(i):
        return ("bank", i)

    P.dma("sp", "misc", 2, lambda e: e.dma_start(out=ident[:, :], in_=ident_d.ap()), writes=["ident"])
    P.dma("sp", "misc", 2, lambda e: e.dma_start(out=maskT[:, :], in_=mask_d.ap()), writes=["maskT"])
    P.op("dve", lambda e: e.memset(ones[:, :], 1.0), writes=["ones"])
    P.op("dve", lambda e: e.memset(onesb[:, :], 1.0), writes=["onesb"])
    P.op("dve", lambda e: e.memset(epsn[:, :], EPS), writes=["epsn"])
    P.op("dve", lambda e: e.memset(epsl[:, :], LN_EPS), writes=["epsl"])
    P.op("dve", lambda e: e.tensor_copy(out=identb[:, :], in_=ident[:, :]), reads=["ident"], writes=["identb"])
    for i in range(2):
        P.op("dve", lambda e, i=i: e.memset(vh[i][:, :, 128:129], 1.0), writes=[("vh1", i)])

    trs = {"n": 0}

    def tr_load(src_ap, r, dst_ap, tag):
        i = trs["n"]
        trs["n"] += 1
        prm = prms[i % 2]
        pb = 4 + i % 4
        P.dma("sp", "misc", 4, lambda e: e.dma_start(out=prm[0:r, :], in_=src_ap), writes=[("prm", i % 2)])
        P.op("pe", lambda e: e.transpose(ps[pb][:, 0:r], prm[0:r, :], ident[0:r, 0:r]),
             reads=[("prm", i % 2), "ident"], writes=[bank(pb)])
        P.op("dve", lambda e: e.tensor_copy(out=dst_ap, in_=ps[pb][:, 0:r]), reads=[bank(pb)], writes=[tag])

    tr_load(norm1_g.ap().rearrange("l (c p) -> (l c) p", p=128), 64, g1c[:, :], "g1c")
    tr_load(norm2_g.ap().rearrange("l (c p) -> (l c) p", p=128), 64, g2c[:, :], "g2c")
    tr_load(final_g.ap().rearrange("(c p) -> c p", p=128), 16, gfc[:, :], "gfc")
    tr_load(conv_b.ap().rearrange("l (c p) -> (l c) p", p=128), 32, cbc[:, :], "cbc")
    tr_load(conv_ln_g.ap().rearrange("l (c p) -> (l c) p", p=128), 32, lgc[:, :], "lgc")
    tr_load(conv_ln_b.ap().rearrange("l (c p) -> (l c) p", p=128), 32, lbc[:, :], "lbc")
    tr_load(subln_g.ap(), 4, sgc[:, :], "sgc")
    for l in range(n_layers):
        lam_init = 0.8 - 0.6 * math.exp(-0.3 * l)
        P.op("dve", lambda e, l=l, v=1.0 - lam_init: e.tensor_scalar(
            sgc[:, l:l + 1], sgc[:, l:l + 1], v, None, ALU.mult), reads=["sgc"], writes=["sgc"])
        for c in range(8):
            o = (l * 8 + c) * KCV
            tr_load(conv_w.ap()[l, :, c * 128:(c + 1) * 128], KCV, cwc[:, o:o + KCV], "cwc")
    for j in range(4):
        P.dma("sp", "misc", 2, lambda e, j=j: e.dma_start(
            out=lamrow[0:1, j * 256:(j + 1) * 256],
            in_=lam_d[j].ap().rearrange("l d -> (l d)").unsqueeze(0)), writes=["lamrow"])
    for j in range(2):
        P.op("pe", lambda e, j=j: e.matmul(ps[6][:, :], lhsT=ones[0:1, :], rhs=lamrow[0:1, j * 512:(j + 1) * 512],
                                           start=True, stop=True),
             reads=["ones", "lamrow"], writes=[bank(6)])
        P.op("dve", lambda e, j=j: e.tensor_copy(out=lamb[:, j * 512:(j + 1) * 512], in_=ps[6][:, :]),
             reads=[bank(6)], writes=["lamb"])
    for j in range(2):
        P.op("dve", lambda e, j=j: e.tensor_tensor(
            out=lamb[:, j * 512:j * 512 + 256], in0=lamb[:, j * 512:j * 512 + 256],
            in1=lamb[:, j * 512 + 256:j * 512 + 512], op=ALU.mult), reads=["lamb"], writes=["lamb"])
        P.op("dve", lambda e, j=j: e.reduce_sum(
            out=lame[:, j * 4:(j + 1) * 4],
            in_=lamb[:, j * 512:j * 512 + 256].rearrange("p (l d) -> p l d", d=64), axis=AX.X),
            reads=["lamb"], writes=["lame"])
    P.op("act", lambda e: e.activation(out=lame[:, 8:16], in_=lame[:, 0:8], func=AF.Exp),
         reads=["lame"], writes=["lame"])
    P.op("dve", lambda e: e.tensor_tensor(out=neglam[:, :], in0=lame[:, 12:16], in1=lame[:, 8:12], op=ALU.subtract),
         reads=["lame"], writes=["neglam"])
    for l in range(n_layers):
        lam_init = 0.8 - 0.6 * math.exp(-0.3 * l)
        P.op("dve", lambda e, l=l, v=-lam_init: e.tensor_scalar(
            neglam[:, l:l + 1], neglam[:, l:l + 1], v, None, ALU.add), reads=["neglam"], writes=["neglam"])

    xrow = [big32[:, 0:2048], big32[:, 2048:4096]]
    stg = [big32[:, 4096:6144], big32[:, 6144:8192]]
    for tb in range(16):
        sl = tb % 2
        P.dma("sp", "xl", 3, lambda e, tb=tb, sl=sl: e.dma_start(
            out=xrow[sl], in_=x_in.ap()[tb * 128:(tb + 1) * 128, :]), writes=[("xrow", sl)])
        for g4 in range(4):
            b = (tb * 4 + g4) % 4
            for j in range(4):
                kc = g4 * 4 + j
                P.op("pe", lambda e, b=b, j=j, kc=kc, sl=sl: e.transpose(
                    ps[b][:, j * 128:(j + 1) * 128], xrow[sl][:, kc * 128:(kc + 1) * 128], ident[:, :]),
                    reads=[("xrow", sl), "ident"], writes=[bank(b)], signal=(j == 3))
            eng = "act" if g4 % 2 else "dve"
            if eng == "act":
                P.op("act", lambda e, b=b, g4=g4, sl=sl: e.copy(out=stg[sl][:, g4 * 512:(g4 + 1) * 512], in_=ps[b][:, :]),
                     reads=[bank(b)], writes=[("stg", sl, g4)])
            else:
                P.op("dve", lambda e, b=b, g4=g4, sl=sl: e.tensor_copy(out=stg[sl][:, g4 * 512:(g4 + 1) * 512], in_=ps[b][:, :]),
                     reads=[bank(b)], writes=[("stg", sl, g4)])
        P.dma("sp", "xs", 3, lambda e, tb=tb, sl=sl: e.dma_start(
            out=xT_d[:, :, tb * 128:(tb + 1) * 128], in_=stg[sl].rearrange("p (k t) -> p k t", t=128)),
            reads=[("stg", sl, g) for g in range(4)], writes=[("xtb", tb)])

    if debug:
        for kc in range(16):
            tk = P.dma("sp", "dbg", 2, lambda e, kc=kc: e.dma_start(out=dbg_x0.ap()[:, kc, :], in_=xT_d[:, kc, :]),
                  reads=[("xtb", tb) for tb in range(16)], writes=[("dbgx0", kc)])
        P.wait_all("sp", [tk])
    wstate = {"n": 0}

    def wload(src3, ncols_total):
        i = wstate["n"]
        wstate["n"] += 1
        sl = i % NW
        kcn = src3.shape[1]
        ncol = src3.shape[2]
        dst = wb[sl][:, 0:kcn * ncol].rearrange("p (k n) -> p k n", n=ncol)
        P.dma("pool", "w", NW, lambda e: e.dma_start(out=dst, in_=src3), writes=[("wb", sl)])
        return sl, dst

    def xkeys(s, kc):
        return [("x", kc, s)] + [("xtb", tb) for tb in range(16)]

    xb_state = {"n": 0}

    def xload(s, kc):
        i = xb_state["n"]
        xb_state["n"] += 1
        sl = i % NXB
        P.dma("sp", "xl", 3, lambda e: e.dma_start(out=xbuf[sl][:, :], in_=xT_d[:, kc, s * TH:(s + 1) * TH]),
              reads=xkeys(s, kc), writes=[("xb", sl)])
        return sl

    def norm_stats(s):
        pend = [xload(s, 0), xload(s, 1)]
        for kc in range(16):
            sl = pend.pop(0)
            if kc + 2 < 16:
                pend.append(xload(s, kc + 2))
            for tt in range(2):
                q = (kc * 2 + tt) % 2
                P.op("act", lambda e, sl=sl, tt=tt, q=q: e.activation(
                    out=sqbb[q][:, :], in_=xbuf[sl][:, tt * NT:(tt + 1) * NT], func=AF.Square),
                    reads=[("xb", sl)], writes=[("sqbb", q)])
                P.op("pe", lambda e, tt=tt, q=q, kc=kc: e.matmul(
                    ps[4 + tt][:, :], lhsT=onesb[:, :], rhs=sqbb[q][:, :], start=(kc == 0), stop=(kc == 15)),
                    reads=["onesb", ("sqbb", q)], writes=[bank(4 + tt)], signal=True)
        for tt in range(2):
            P.op("act", lambda e, tt=tt: e.activation(
                out=nrstd[:, tt * NT:(tt + 1) * NT], in_=ps[4 + tt][:, :], func=AF.Sqrt,
                bias=epsn[:, 0:1], scale=1.0 / D), reads=[bank(4 + tt), "epsn"], writes=[("nrstd", tt)])
            P.op("dve", lambda e, tt=tt: e.reciprocal(
                out=nrstd[:, tt * NT:(tt + 1) * NT], in_=nrstd[:, tt * NT:(tt + 1) * NT]),
                reads=[("nrstd", tt)], writes=[("nrstd", tt)])

    def norm_apply_h(s, gcols):
        pend = [xload(s, 0), xload(s, 1)]
        for kc in range(16):
            sl = pend.pop(0)
            if kc + 2 < 16:
                pend.append(xload(s, kc + 2))
            P.op("dve", lambda e, sl=sl, kc=kc: e.scalar_tensor_tensor(
                out=hm[:, kc, :], in0=xbuf[sl][:, :], scalar=gcols[:, kc:kc + 1], in1=nrstd[:, :],
                op0=ALU.mult, op1=ALU.mult),
                reads=[("xb", sl), ("nrstd", 0), ("nrstd", 1), "g1c", "g2c"], writes=[("hm", kc)])

    def wsrc(wt, l, kc0, kcn, c0, ncol):
        return wt.ap()[l].rearrange("(k p) n -> p k n", p=128)[:, kc0:kc0 + kcn, c0:c0 + ncol]

    for l in range(n_layers):
        P.epoch = l
        lam_init = 0.8 - 0.6 * math.exp(-0.3 * l)
        for s in range(2):
            t0 = s * TH
            qT = lambda h: bigb[:, h * TH:(h + 1) * TH]
            qAB = lambda c, h: bigb[:, (c * 8 + h) * TH:(c * 8 + h + 1) * TH]
            qkeys = [("q", h, tt) for h in range(NH) for tt in range(2)] + [("act", fc) for fc in range(16)]
            if l == 0 and s == 0:
                qkeys = qkeys + [("xrow", 0), ("xrow", 1)] + [("stg", sl_, g_) for sl_ in range(2) for g_ in range(4)]
            P.op("pool", lambda e: e.memset(bigb[64:128, 0:8 * TH], 0.0), writes=qkeys)
            P.op("pool", lambda e: e.memset(bigb[0:64, 8 * TH:16 * TH], 0.0), writes=qkeys)
            actT = lambda fc: bigb[:, fc * TH:(fc + 1) * TH]
            norm_stats(s)
            norm_apply_h(s, g1c[:, l * 16:(l + 1) * 16])

            tiles = []
            for j in range(4):
                tiles.append(("k", j, 3072 + j * 256))
            for j in range(4):
                tiles.append(("v", j, 4096 + j * 256))
            for j in range(4):
                tiles.append(("q", j, 2048 + j * 256))
            for j in range(4):
                tiles.append(("g", j, 1024 + j * 256))
                tiles.append(("a", j, j * 256))
            loads = []

            def ensure_loads(upto, tiles=tiles, loads=loads, l=l):
                while len(loads) < min(upto, len(tiles)):
                    kind, j, c0 = tiles[len(loads)]
                    loads.append(wload(wsrc(w_in, l, 0, 16, c0, 256), 256))

            ensure_loads(3)
            gst = {"gi": 0}
            pending_conv = []
            if s == 0:
                for i in range(2):
                    P.op("dve", lambda e, i=i: e.memset(glu[i][:, 0:30], 0.0), writes=[("glu", i)])
            for ti, (kind, j, c0) in enumerate(tiles):
                ensure_loads(ti + 3)
                wsl, wv = loads[ti]
                if kind == "v":
                    for tb in range(8):
                        b = gst["gi"] % 4
                        gst["gi"] += 1
                        for kc in range(16):
                            P.op("pe", lambda e, b=b, kc=kc, tb=tb, wv=wv: e.matmul(
                                ps[b][:, 0:256], lhsT=hm[:, kc, tb * 128:(tb + 1) * 128], rhs=wv[:, kc, :],
                                start=(kc == 0), stop=(kc == 15)),
                                reads=[("hm", kc), ("wb", wsl)], writes=[bank(b)], signal=(kc == 15))
                        q = tb % 2
                        P.op("act", lambda e, b=b, q=q: e.copy(out=vst[q][:, :], in_=ps[b][:, 0:256]),
                             reads=[bank(b)], writes=[("vst", q)])
                        kb = s * 8 + tb
                        P.dma("sp", "kv", 4, lambda e, q=q, kb=kb, j=j: e.dma_start(
                            out=V_d.ap()[2 * j:2 * j + 2, :, kb, :].rearrange("h p d -> p h d"),
                            in_=vst[q][:, :].rearrange("p (h d) -> p h d", d=128)),
                            reads=[("vst", q)], writes=[("V", 2 * j, kb), ("V", 2 * j + 1, kb)])
                    continue
                for oc in range(2):
                    b0 = (gst["gi"] % 2) * 2
                    gst["gi"] += 1
                    for kc in range(16):
                        for tt in range(2):
                            P.op("pe", lambda e, b=b0 + tt, kc=kc, tt=tt, oc=oc, wv=wv: e.matmul(
                                ps[b][:, :], lhsT=wv[:, kc, oc * 128:(oc + 1) * 128],
                                rhs=hm[:, kc, tt * NT:(tt + 1) * NT], start=(kc == 0), stop=(kc == 15)),
                                reads=[("hm", kc), ("wb", wsl)], writes=[bank(b0 + tt)], signal=(kc == 15))
                    if pending_conv:
                        pending_conv.pop(0)()
                    h = 2 * j + oc
                    for tt in range(2):
                        b = b0 + tt
                        if kind == "k":
                            q = (h * 2 + tt) % 2
                            P.op("act", lambda e, b=b, q=q: e.copy(out=kst[q][:, :], in_=ps[b][:, :]),
                                 reads=[bank(b)], writes=[("kst", q)])
                            P.dma("sp", "kv", 4, lambda e, q=q, h=h, tt=tt: e.dma_start(
                                out=kT_d[h, :, t0 + tt * NT:t0 + (tt + 1) * NT], in_=kst[q][:, :]),
                                reads=[("kst", q)], writes=[("K", h, s * 2 + tt)])
                        elif kind == "q":
                            P.op("dve", lambda e, b=b, h=h, tt=tt: e.tensor_copy(
                                out=qAB(0, h)[0:64, tt * NT:(tt + 1) * NT], in_=ps[b][0:64, :]),
                                reads=[bank(b)], writes=[("q", h, tt)])
                            P.op("dve", lambda e, b=b, h=h, tt=tt: e.tensor_copy(
                                out=qAB(1, h)[64:128, tt * NT:(tt + 1) * NT], in_=ps[b][64:128, :]),
                                reads=[bank(b)], writes=[("q", h, tt)])
                        elif kind == "g":
                            P.op("act", lambda e, b=b, tt=tt, oc=oc: e.activation(
                                out=sig2[oc][:, tt * NT:(tt + 1) * NT], in_=ps[b][:, :], func=AF.Sigmoid),
                                reads=[bank(b)], writes=[("sig", oc, tt)])
                        else:
                            gs = h % 2
                            P.op("dve", lambda e, b=b, tt=tt, gs=gs: e.tensor_tensor(
                                out=glu[gs][:, 30 + tt * NT:30 + (tt + 1) * NT], in0=ps[b][:, :],
                                in1=sig2[gs][:, tt * NT:(tt + 1) * NT], op=ALU.mult),
                                reads=[bank(b), ("sig", gs, tt)], writes=[("glu", gs)])
                    if kind == "a":
                        c = h
                        gs = c % 2
                        if s == 1:
                            P.op("dve", lambda e, gs=gs, c=c: e.tensor_copy(out=glu[gs][:, 0:30], in_=halo[:, c, :]),
                                 reads=[("halo", c)], writes=[("glu", gs)])
                        else:
                            P.op("dve", lambda e, gs=gs, c=c: e.tensor_copy(out=halo[:, c, :], in_=glu[gs][:, TH:TH + 30]),
                                 reads=[("glu", gs)], writes=[("halo", c)])
                        wo = (l * 8 + c) * KCV
                        col = l * 8 + c
                        for k in range(KCV):
                            P.op("dve", lambda e, k=k, wo=wo: e.tensor_scalar(
                                dg[:, k, :], identb[:, :], cwc[:, wo + k:wo + k + 1], None, ALU.mult),
                                reads=["identb", "cwc"], writes=[("dg", k)])

                        def conv_task(c=c, gs=gs, col=col):
                            b0 = (gst["gi"] % 2) * 2
                            gst["gi"] += 1
                            for tt in range(2):
                                for k in range(KCV):
                                    P.op("pe", lambda e, b=b0 + tt, k=k, tt=tt: e.matmul(
                                        ps[b][:, :], lhsT=dg[:, k, :], rhs=glu[gs][:, k + tt * NT:k + (tt + 1) * NT],
                                        start=(k == 0), stop=(k == KCV - 1)),
                                        reads=[("dg", k), ("glu", gs)], writes=[bank(b0 + tt)], signal=(k == KCV - 1))
                            for tt in range(2):
                                q = tt
                                P.op("act", lambda e, b=b0 + tt, tt=tt: e.activation(
                                    out=cacc[gs][:, tt * NT:(tt + 1) * NT], in_=ps[b][:, :], func=AF.Identity,
                                    bias=cbc[:, col:col + 1], scale=1.0),
                                    reads=[bank(b0 + tt), "cbc"], writes=[("cacc", gs)])
                                P.op("act", lambda e, tt=tt, q=q: e.activation(
                                    out=sqb[q][:, :], in_=cacc[gs][:, tt * NT:(tt + 1) * NT], func=AF.Square),
                                    reads=[("cacc", gs)], writes=[("sq", q)])
                                P.op("pe", lambda e, tt=tt: e.matmul(
                                    ps[4 + tt][:, :], lhsT=ones[:, :], rhs=cacc[gs][:, tt * NT:(tt + 1) * NT],
                                    start=(c == 0), stop=(c == 7)),
                                    reads=["ones", ("cacc", gs)], writes=[bank(4 + tt)], signal=True)
                                P.op("pe", lambda e, tt=tt, q=q: e.matmul(
                                    ps[6 + tt][:, :], lhsT=ones[:, :], rhs=sqb[q][:, :],
                                    start=(c == 0), stop=(c == 7)),
                                    reads=["ones", ("sq", q)], writes=[bank(6 + tt)], signal=True)
                            P.dma("sp", "cs", 2, lambda e: e.dma_start(out=c32_d[:, c, :], in_=cacc[gs][:, :]),
                                  reads=[("cacc", gs)], writes=[("c32", c)])

                        pending_conv.append(conv_task)
            while pending_conv:
                pending_conv.pop(0)()
            for tt in range(2):
                tsl = slice(tt * NT, (tt + 1) * NT)
                P.op("dve", lambda e, tt=tt, tsl=tsl: e.tensor_scalar(
                    lnmu[:, tsl], ps[4 + tt][:, :], 1.0 / CW, None, ALU.mult),
                    reads=[bank(4 + tt)], writes=[("lnmu", tt)])
                P.op("dve", lambda e, tt=tt, tsl=tsl: e.tensor_tensor(
                    out=lnrs[:, tsl], in0=lnmu[:, tsl], in1=lnmu[:, tsl], op=ALU.mult),
                    reads=[("lnmu", tt)], writes=[("lnrs", tt)])
                P.op("dve", lambda e, tt=tt, tsl=tsl: e.scalar_tensor_tensor(
                    out=lnrs[:, tsl], in0=ps[6 + tt][:, :], scalar=1.0 / CW, in1=lnrs[:, tsl],
                    op0=ALU.mult, op1=ALU.subtract),
                    reads=[bank(6 + tt), ("lnrs", tt)], writes=[("lnrs", tt)])
                P.op("act", lambda e, tt=tt, tsl=tsl: e.activation(
                    out=lnrs[:, tsl], in_=lnrs[:, tsl], func=AF.Sqrt, bias=epsl[:, 0:1], scale=1.0),
                    reads=[("lnrs", tt), "epsl"], writes=[("lnrs", tt)])
                P.op("dve", lambda e, tt=tt, tsl=tsl: e.reciprocal(out=lnrs[:, tsl], in_=lnrs[:, tsl]),
                     reads=[("lnrs", tt)], writes=[("lnrs", tt)])
            pend = []

            def cload(c):
                i = xb_state["n"]
                xb_state["n"] += 1
                sl = i % NXB
                P.dma("sp", "xl", 3, lambda e: e.dma_start(out=xbuf[sl][:, :], in_=c32_d[:, c, :]),
                      reads=[("c32", c)], writes=[("xb", sl)])
                return sl

            pend = [cload(0), cload(1)]
            for c in range(8):
                sl = pend.pop(0)
                if c + 2 < 8:
                    pend.append(cload(c + 2))
                P.op("dve", lambda e, sl=sl: e.tensor_tensor(out=t1b[:, :], in0=xbuf[sl][:, :], in1=lnmu[:, :], op=ALU.subtract),
                     reads=[("xb", sl), ("lnmu", 0), ("lnmu", 1)], writes=[("sig", 1, 0), ("sig", 1, 1)])
                P.op("dve", lambda e: e.tensor_tensor(out=t1b[:, :], in0=t1b[:, :], in1=lnrs[:, :], op=ALU.mult),
                     reads=[("sig", 1, 0), ("sig", 1, 1), ("lnrs", 0), ("lnrs", 1)], writes=[("sig", 1, 0), ("sig", 1, 1)])
                col = l * 8 + c
                P.op("act", lambda e, c=c, col=col: e.activation(
                    out=hm[:, c, :], in_=t1b[:, :], func=AF.Silu, bias=lbc[:, col:col + 1], scale=lgc[:, col:col + 1]),
                    reads=[("sig", 1, 0), ("sig", 1, 1), "lbc", "lgc"], writes=[("hm", c)])

            nkb_all = (s + 1) * 8

            def kvload(h, sl):
                P.dma("sp", "kvl", 4, lambda e: e.dma_start(out=kh[sl][:, 0:nkb_all * 128], in_=kT_d[h, :, 0:nkb_all * 128]),
                      reads=[("K", h, i) for i in range((s + 1) * 2)], writes=[("kh", sl)])
                P.dma("sp", "kvl", 4, lambda e: e.dma_start(out=vh[sl][:, 0:nkb_all, 0:128], in_=V_d.ap()[h, :, 0:nkb_all, :]),
                      reads=[("V", h, kb) for kb in range(nkb_all)], writes=[("vh", sl)])

            steps = []
            for h in range(NH):
                for tt in range(2):
                    for c in range(2):
                        for kb in range(s * 8 + tt * 4 + 4):
                            steps.append((h, tt, c, kb))
            ep_pending = []

            def emit_qk(i):
                h, tt, c, kb = steps[i]
                hs = h % 2
                qb0 = s * 8 + tt * 4
                jd = kb - qb0
                n0 = max(jd, 0) * 128
                n = NT - n0
                sb_ = (0, 1, 7)[i % 3]
                pt = i % NPT
                diag = jd >= 0
                P.op("pe", lambda e: e.matmul(
                    ps[sb_][:, 0:n], lhsT=kh[hs][:, kb * 128:(kb + 1) * 128],
                    rhs=qAB(c, h)[:, tt * NT + n0:(tt + 1) * NT], start=True, stop=(not diag)),
                    reads=[("kh", hs), ("q", h, tt)], writes=[bank(sb_)], signal=(not diag))
                if diag:
                    P.op("pe", lambda e: e.matmul(
                        ps[sb_][:, 0:128], lhsT=identb[:, :], rhs=maskT[:, :], start=False, stop=True),
                        reads=["identb", "maskT"], writes=[bank(sb_)], signal=True)
                P.op("act", lambda e: e.activation(
                    out=ptb[pt][:, 0:n], in_=ps[sb_][:, 0:n], func=AF.Exp, scale=0.125),
                    reads=[bank(sb_)], writes=[("pt", pt)])

            def emit_av(i):
                h, tt, c, kb = steps[i]
                hs = h % 2
                qb0 = s * 8 + tt * 4
                n0 = max(kb - qb0, 0) * 128
                pt = i % NPT
                for qi in range(4):
                    if qi * 128 < n0:
                        continue
                    last = (kb == qb0 + qi)
                    P.op("pe", lambda e, qi=qi, last=last: e.matmul(
                        ps[2 + qi][:, 0:129], lhsT=ptb[pt][:, qi * 128 - n0:(qi + 1) * 128 - n0],
                        rhs=vh[hs][:, kb, 0:129], start=(kb == 0), stop=last),
                        reads=[("pt", pt), ("vh", hs), ("vh1", hs)], writes=[bank(2 + qi)], signal=last)
                if kb != qb0 + 3:
                    return
                for qi in range(4):
                    ob = 2 + qi
                    P.op("dve", lambda e, qi=qi, ob=ob: e.tensor_copy(out=osb[c][:, qi, 0:129], in_=ps[ob][:, 0:129]),
                         reads=[bank(ob)], writes=[("osb", c)])
                if c == 0:
                    return
                while ep_pending:
                    ep_pending.pop(0)[1]()
                bc = lambda ap: ap.unsqueeze(2).to_broadcast([128, 4, 128])
                P.op("dve", lambda e: e.reciprocal(out=sm[0][:, 0:4], in_=osb[0][:, :, 128]),
                     reads=[("osb", 0)], writes=["sm0"])
                P.op("dve", lambda e: e.reciprocal(out=sm[1][:, 0:4], in_=osb[1][:, :, 128]),
                     reads=[("osb", 1)], writes=["sm1"])
                P.op("dve", lambda e: e.tensor_scalar(sm[1][:, 0:4], sm[1][:, 0:4], neglam[:, l:l + 1], None, ALU.mult),
                     reads=["sm1", "neglam"], writes=["sm1"])
                P.op("dve", lambda e: e.tensor_tensor(out=osb[0][:, :, 0:128], in0=osb[0][:, :, 0:128],
                                                      in1=bc(sm[0][:, 0:4]), op=ALU.mult),
                     reads=[("osb", 0), "sm0"], writes=[("osb", 0)])
                P.op("dve", lambda e: e.tensor_tensor(out=osb[1][:, :, 0:128], in0=osb[1][:, :, 0:128],
                                                      in1=bc(sm[1][:, 0:4]), op=ALU.mult),
                     reads=[("osb", 1), "sm1"], writes=[("osb", 1)])
                P.op("dve", lambda e: e.tensor_tensor(out=o4[:, :, :], in0=osb[0][:, :, 0:128], in1=osb[1][:, :, 0:128], op=ALU.add),
                     reads=[("osb", 0), ("osb", 1)], writes=["o4"])
                P.op("dve", lambda e: e.tensor_tensor(out=sq4[:, :, :], in0=o4[:, :, :], in1=o4[:, :, :], op=ALU.mult),
                     reads=["o4"], writes=["sq4"])
                P.op("dve", lambda e: e.reduce_sum(out=sm[1][:, 4:8], in_=sq4[:, :, :], axis=AX.X),
                     reads=["sq4"], writes=["ss"])

                def part1c():
                    P.op("act", lambda e: e.activation(out=sm[1][:, 4:8], in_=sm[1][:, 4:8], func=AF.Ln,
                                                       bias=epsn[:, 0:1], scale=1.0 / 128),
                         reads=["ss", "epsn"], writes=["ss"])
                    P.op("act", lambda e: e.activation(out=sm[1][:, 4:8], in_=sm[1][:, 4:8], func=AF.Exp, scale=-0.5),
                         reads=["ss"], writes=["ss"])
                    P.op("dve", lambda e: e.tensor_tensor(out=on4[:, :, :], in0=o4[:, :, :], in1=bc(sm[1][:, 4:8]), op=ALU.mult),
                         reads=["o4", "ss"], writes=["on4"])

                def part2(h=h, tt=tt):
                    for qi in range(4):
                        P.op("pe", lambda e, qi=qi: e.transpose(ps[6][:, qi * 128:(qi + 1) * 128], on4[:, qi, :], ident[:, :]),
                             reads=["on4", "ident"], writes=[bank(6)], signal=(qi == 3))
                    P.op("dve", lambda e: e.tensor_scalar(
                        hm[:, 8 + h, tt * NT:(tt + 1) * NT], ps[6][:, :], sgc[:, l:l + 1], None, ALU.mult),
                        reads=[bank(6), "sgc"], writes=[("hm", 8 + h)])

                ep_pending.append((i + 8, part1c))
                ep_pending.append((i + 12, part2))

            kvload(0, 0)
            LA = 3
            for i0 in range(LA):
                emit_qk(i0)
            for i in range(len(steps)):
                h, tt, c, kb = steps[i]
                if tt == 0 and c == 0 and kb == 0 and h + 1 < NH:
                    kvload(h + 1, (h + 1) % 2)
                emit_av(i)
                while ep_pending and ep_pending[0][0] <= i:
                    ep_pending.pop(0)[1]()
                if i + LA < len(steps):
                    emit_qk(i + LA)
            while ep_pending:
                ep_pending.pop(0)[1]()

            if debug and l == 0:
                out_dbg = []
                out_dbg.append(P.dma("sp", "dbg", 2, lambda e, s=s: e.dma_start(out=dbg_mix.ap()[s], in_=hm[:, :, :]),
                      reads=[("hm", c) for c in range(16)], writes=[("dbgmix", s)]))
                out_dbg.append(P.dma("sp", "dbg", 2, lambda e, s=s: e.dma_start(out=dbg_q.ap()[s], in_=bigb[:, 0:8 * TH]),
                      reads=[("q", h, tt) for h in range(8) for tt in range(2)], writes=[("dbgq", s)]))
                P.wait_all("sp", out_dbg)
            tiles = [(j, j * 256) for j in range(8)]
            loads = []

            def ensure_loads_o(upto, tiles=tiles, loads=loads, l=l):
                while len(loads) < min(upto, len(tiles)):
                    j, c0 = tiles[len(loads)]
                    loads.append(wload(wsrc(w_out, l, 0, 16, c0, 256), 256))

            def resid_gemm(ntiles, ensure, loads, kcn, rhs_fn, rhs_keys, gi0=0):
                gi = gi0
                for ti in range(ntiles):
                    ensure(ti + 3)
                    wsl, wv = loads[ti]
                    noc = wv.shape[2] // 128
                    for oc in range(noc):
                        ocg = ti * noc + oc
                        xsl = xload(s, ocg)
                        b0 = (gi % 2) * 2
                        gi += 1
                        for kc in range(kcn):
                            for tt in range(2):
                                P.op("pe", lambda e, b=b0 + tt, kc=kc, tt=tt, oc=oc, wv=wv: e.matmul(
                                    ps[b][:, :], lhsT=wv[:, kc, oc * 128:(oc + 1) * 128],
                                    rhs=rhs_fn(kc)[:, tt * NT:(tt + 1) * NT], start=(kc == 0), stop=(kc == kcn - 1)),
                                    reads=[rhs_keys(kc), ("wb", wsl)], writes=[bank(b0 + tt)], signal=(kc == kcn - 1))
                        for tt in range(2):
                            P.op("dve", lambda e, b=b0 + tt, tt=tt, xsl=xsl: e.tensor_tensor(
                                out=xbuf[xsl][:, tt * NT:(tt + 1) * NT], in0=xbuf[xsl][:, tt * NT:(tt + 1) * NT],
                                in1=ps[b][:, :], op=ALU.add), reads=[bank(b0 + tt), ("xb", xsl)], writes=[("xb", xsl)])
                        P.dma("sp", "xs", 3, lambda e, xsl=xsl, ocg=ocg: e.dma_start(
                            out=xT_d[:, ocg, s * TH:(s + 1) * TH], in_=xbuf[xsl][:, :]),
                            reads=[("xb", xsl)], writes=[("x", ocg, s)])

            ensure_loads_o(3)
            resid_gemm(8, ensure_loads_o, loads, 16, lambda kc: hm[:, kc, :], lambda kc: ("hm", kc))

            norm_stats(s)
            norm_apply_h(s, g2c[:, l * 16:(l + 1) * 16])

            for fh in range(2):
                f0 = fh * 2816
                tiles = []
                for j in range(11):
                    tiles.append(("g", j))
                    tiles.append(("u", j))
                loads = []

                def ensure_loads_f(upto, tiles=tiles, loads=loads, l=l, f0=f0):
                    while len(loads) < min(upto, len(tiles)):
                        kind, j = tiles[len(loads)]
                        wt = w_gate if kind == "g" else w_up
                        loads.append(wload(wsrc(wt, l, 0, 16, f0 + j * 256, 256), 256))

                ensure_loads_f(3)
                for j in range(11):
                    ensure_loads_f(2 * j + 4)
                    gsl, gv = loads[2 * j]
                    usl, uv = loads[2 * j + 1]
                    for oc in range(2):
                        fc = 2 * j + oc
                        bb = (fc % 2) * 4
                        for (wsl, wv, bo) in ((gsl, gv, 0), (usl, uv, 2)):
                            for kc in range(16):
                                for tt in range(2):
                                    P.op("pe", lambda e, b=bb + bo + tt, kc=kc, tt=tt, oc=oc, wv=wv: e.matmul(
                                        ps[b][:, :], lhsT=wv[:, kc, oc * 128:(oc + 1) * 128],
                                        rhs=hm[:, kc, tt * NT:(tt + 1) * NT], start=(kc == 0), stop=(kc == 15)),
                                        reads=[("hm", kc), ("wb", wsl)], writes=[bank(bb + bo + tt)], signal=(kc == 15))
                        for tt in range(2):
                            P.op("act", lambda e, b=bb + tt, tt=tt: e.activation(
                                out=sig[:, tt * NT:(tt + 1) * NT], in_=ps[b][:, :], func=AF.Silu),
                                reads=[bank(bb + tt)], writes=[("sig", 0, tt)])
                            P.op("dve", lambda e, b=bb + 2 + tt, tt=tt, fc=fc: e.tensor_tensor(
                                out=actT(fc)[:, tt * NT:(tt + 1) * NT], in0=ps[b][:, :],
                                in1=sig[:, tt * NT:(tt + 1) * NT], op=ALU.mult),
                                reads=[bank(bb + 2 + tt), ("sig", 0, tt)], writes=[("act", fc)])
                tiles = list(range(16))
                loads = []

                def ensure_loads_d(upto, tiles=tiles, loads=loads, l=l, fh=fh):
                    while len(loads) < min(upto, len(tiles)):
                        oc = tiles[len(loads)]
                        loads.append(wload(wsrc(w_down, l, fh * 22, 22, oc * 128, 128), 128))

                ensure_loads_d(3)
                resid_gemm(16, ensure_loads_d, loads, 22, lambda kc: actT(kc), lambda kc: ("act", kc))

    P.epoch = n_layers - 1
    rowst = big32[:, 0:8192]
    out_toks = []
    for s in range(2):
        norm_stats(s)
        for tt in range(2):
            pend = []

            def fload(kc, s=s, tt=tt):
                i = xb_state["n"]
                xb_state["n"] += 1
                sl = i % NXB
                P.dma("sp", "xl", 3, lambda e: e.dma_start(
                    out=xbuf[sl][:, 0:NT], in_=xT_d[:, kc, s * TH + tt * NT:s * TH + (tt + 1) * NT]),
                    reads=xkeys(s, kc), writes=[("xb", sl)])
                return sl

            pend = [fload(0), fload(1)]
            for kc in range(16):
                sl = pend.pop(0)
                if kc + 2 < 16:
                    pend.append(fload(kc + 2))
                q = kc % 2
                P.op("dve", lambda e, sl=sl, kc=kc, q=q, tt=tt: e.scalar_tensor_tensor(
                    out=sqb[q][:, :], in0=xbuf[sl][:, 0:NT], scalar=gfc[:, kc:kc + 1],
                    in1=nrstd[:, tt * NT:(tt + 1) * NT], op0=ALU.mult, op1=ALU.mult),
                    reads=[("xb", sl), ("nrstd", tt), "gfc"], writes=[("sq", q)])
                b = kc % 4
                for j in range(4):
                    P.op("pe", lambda e, b=b, j=j, q=q: e.transpose(
                        ps[b][:, j * 128:(j + 1) * 128], sqb[q][:, j * 128:(j + 1) * 128], ident[:, :]),
                        reads=[("sq", q), "ident"], writes=[bank(b)], signal=(j == 3))
                if kc % 2:
                    P.op("act", lambda e, b=b, kc=kc: e.copy(
                        out=rowst.rearrange("p (j f) -> p j f", f=D)[:, :, kc * 128:(kc + 1) * 128],
                        in_=ps[b][:, :].rearrange("p (j f) -> p j f", f=128)),
                        reads=[bank(b)], writes=[("rowst", kc)])
                else:
                    P.op("dve", lambda e, b=b, kc=kc: e.tensor_copy(
                        out=rowst.rearrange("p (j f) -> p j f", f=D)[:, :, kc * 128:(kc + 1) * 128],
                        in_=ps[b][:, :].rearrange("p (j f) -> p j f", f=128)),
                        reads=[bank(b)], writes=[("rowst", kc)])
            tok0 = s * TH + tt * NT
            out_toks.append(P.dma("sp", "ys", 2, lambda e, tok0=tok0: e.dma_start(
                out=y_out.ap()[tok0:tok0 + NT, :].rearrange("(j p) f -> p j f", p=128),
                in_=rowst.rearrange("p (j f) -> p j f", f=D)),
                reads=[("rowst", kc) for kc in range(16)], writes=[("y", s, tt)]))
    P.wait_all("sp", out_toks)

    nc.allow_low_precision("bf16 matmul operands with fp32 PSUM accumulation")
    with ExitStack() as es:
        sems = {}
        for i, key in enumerate(sorted(P.semkeys, key=str)):
            sems[key] = es.enter_context(nc.semaphore(f"s{i}"))
        block = es.enter_context(nc.Block())

        def replay(eng_name):
            def body(e):
                for (fn, waits, inc) in P.ops[eng_name]:
                    for (key, val) in waits:
                        e.wait_ge(sems[key], val)
                    if fn is None:
                        continue
                    ins = fn(e)
                    if inc is not None:
                        ins.then_inc(sems[inc[0]], inc[1])
            return body

        block.tensor(replay("pe"))
        block.scalar(replay("act"))
        block.vector(replay("dve"))
        block.gpsimd(replay("pool"))
        block.sync(replay("sp"))
    return nc


_CACHE = {}


def _consts():
    ident = np.eye(128, dtype=np.float32)
    k = np.arange(128)[:, None]
    q = np.arange(128)[None, :]
    mask = np.where(q >= k, 0.0, -30000.0).astype(np.float32).astype(ml_dtypes.bfloat16)
    return ident, mask


def kernel(**inputs):
    if "nc" not in _CACHE:
        _CACHE["nc"] = build(DEPTH)
    nc = _CACHE["nc"]
    ident, mask = _consts()
    shared = {}
    for k, v in inputs.items():
        if k == "x":
            continue
        shared[k] = np.ascontiguousarray(np.asarray(v, dtype=np.float32))
    shared["c_ident"] = ident
    shared["c_mask"] = mask
    x = np.asarray(inputs["x"], dtype=np.float32)
    in_maps = []
    for b in range(NCORES):
        m = dict(shared)
        m["x"] = np.ascontiguousarray(x[b])
        in_maps.append(m)
    res = run_bass_kernel_spmd(nc, in_maps, core_ids=list(range(NCORES)))
    return np.stack([np.asarray(res.results[b]["y"], dtype=np.float32) for b in range(NCORES)], axis=0)
```

```python
import math
import types
from contextlib import ExitStack
import numpy as np
import ml_dtypes
import concourse.bass as bass
import concourse.mybir as mybir
from concourse.bass_utils import run_bass_kernel_spmd

F32 = mybir.dt.float32
BF16 = mybir.dt.bfloat16
ALU = mybir.AluOpType
AF = mybir.ActivationFunctionType
AX = mybir.AxisListType

D = 2048
S = 2048
DEPTH = 4
CW = 1024
NH = 8
KCV = 31
DFF = 5632
INW = 5120
TH = 1024
NT = 512
EPS = 1e-6
LN_EPS = 1e-5
NCORES = 4
ENG = ("pe", "act", "dve", "pool", "sp")


def _freeze(fn):
    if fn is None or fn.__closure__ is None:
        return fn
    cells = []
    for c in fn.__closure__:
        try:
            cells.append(types.CellType(c.cell_contents))
        except ValueError:
            cells.append(c)
    return types.FunctionType(fn.__code__, fn.__globals__, fn.__name__, fn.__defaults__, tuple(cells))


class Prog:
    def __init__(self):
        self.ops = {e: [] for e in ENG}
        self.cnt = {}
        self.epoch = 0
        self.last_write = {}
        self.readers = {}
        self.waited = {e: {} for e in ENG}
        self.dma_cnt = {}
        self.dma_rr = {}
        self.semkeys = set()

    def _deps(self, reads, writes):
        deps = set()
        for r in reads:
            lw = self.last_write.get(r)
            if lw:
                deps.add(lw)
        for w in writes:
            lw = self.last_write.get(w)
            if lw:
                deps.add(lw)
            deps |= self.readers.get(w, set())
        return deps

    def _waits(self, eng, deps):
        waits = []
        for (key, val) in sorted(deps, key=lambda t: (str(t[0]), t[1])):
            if eng == "pe" and key[0] == "pe":
                continue
            if self.waited[eng].get(key, 0) >= val:
                continue
            self.waited[eng][key] = val
            waits.append((key, val))
        return waits

    def _finish(self, tok, reads, writes):
        for w in writes:
            self.last_write[w] = tok
            self.readers[w] = set()
        for r in reads:
            self.readers.setdefault(r, set()).add(tok)

    def op(self, eng, fn, reads=(), writes=(), signal=True):
        waits = self._waits(eng, self._deps(reads, writes))
        key = (eng, self.epoch)
        self.semkeys.add(key)
        if signal:
            self.cnt[key] = self.cnt.get(key, 0) + 1
            tok = (key, self.cnt[key])
            inc = (key, 1)
        else:
            tok = (key, self.cnt.get(key, 0) + 1)
            inc = None
        self.ops[eng].append((_freeze(fn), waits, inc))
        self._finish(tok, reads, writes)
        return tok

    def dma(self, queue, stream, nsem, fn, reads=(), writes=()):
        idx = self.dma_rr.get(stream, 0)
        self.dma_rr[stream] = idx + 1
        key = ("dma", stream, idx % nsem)
        self.semkeys.add(key)
        deps = self._deps(reads, writes)
        prev = self.dma_cnt.get(key, 0)
        if prev:
            deps.add((key, prev))
        waits = self._waits(queue, deps)
        self.dma_cnt[key] = prev + 16
        tok = (key, prev + 16)
        self.ops[queue].append((_freeze(fn), waits, (key, 16)))
        self._finish(tok, reads, writes)
        return tok

    def wait_all(self, eng, toks):
        waits = self._waits(eng, set(toks))
        self.ops[eng].append((None, waits, None))


def build(n_layers=DEPTH, debug=False):
    nc = bass.Bass("TRN2", target_bir_lowering=False)
    P = Prog()

    def din(name, shape, dt=F32):
        return nc.dram_tensor(name, list(shape), dt, kind="ExternalInput")

    x_in = din("x", [S, D])
    norm1_g = din("norm1_g", [DEPTH, D])
    w_in = din("w_in", [DEPTH, D, INW])
    conv_w = din("conv_w", [DEPTH, KCV, CW])
    conv_b = din("conv_b", [DEPTH, CW])
    conv_ln_g = din("conv_ln_g", [DEPTH, CW])
    conv_ln_b = din("conv_ln_b", [DEPTH, CW])
    lam_d = [din(n, [DEPTH, 64]) for n in ("lam_q1", "lam_k1", "lam_q2", "lam_k2")]
    subln_g = din("subln_g", [DEPTH, 128])
    w_out = din("w_out", [DEPTH, D, D])
    norm2_g = din("norm2_g", [DEPTH, D])
    w_gate = din("w_gate", [DEPTH, D, DFF])
    w_up = din("w_up", [DEPTH, D, DFF])
    w_down = din("w_down", [DEPTH, DFF, D])
    final_g = din("final_g", [D])
    ident_d = din("c_ident", [128, 128])
    mask_d = din("c_mask", [128, 128], BF16)
    y_out = nc.dram_tensor("y", [S, D], F32, kind="ExternalOutput")

    skind = "ExternalOutput" if debug else "Internal"
    xT_d = nc.dram_tensor("xT_d", [128, 16, S], F32, kind=skind)
    kT_d = nc.dram_tensor("kT_d", [NH, 128, S], BF16, kind=skind)
    V_d = nc.dram_tensor("V_d", [NH, 128, 16, 128], BF16, kind=skind)
    c32_d = nc.dram_tensor("c32_d", [128, 8, TH], F32, kind=skind)
    if debug:
        dbg_mix = nc.dram_tensor("dbg_mix", [2, 128, 16, TH], BF16, kind="ExternalOutput")
        dbg_q = nc.dram_tensor("dbg_q", [2, 128, 8 * TH], BF16, kind="ExternalOutput")
        dbg_x0 = nc.dram_tensor("dbg_x0", [128, 16, S], F32, kind="ExternalOutput")

    def sb(name, shape, dt):
        return nc.alloc_sbuf_tensor(name, list(shape), dt)

    hm = sb("hm", [128, 16, TH], BF16)
    big32 = sb("big32", [128, 11264], F32)
    bigb = big32[:, :].bitcast(BF16)
    NW = 4
    wb = [sb(f"wb{i}", [128, 4096], BF16) for i in range(NW)]
    kh = [sb(f"kh{i}", [128, S], BF16) for i in range(2)]
    vh = [sb(f"vh{i}", [128, 16, 132], BF16) for i in range(2)]
    NXB = 3
    xbuf = [sb(f"xb{i}", [128, TH], F32) for i in range(NXB)]
    cab = [sb(f"cab{i}", [128, NT], BF16) for i in range(2)]
    glu = [sb(f"glu{i}", [128, 30 + TH], BF16) for i in range(2)]
    dg = sb("dg", [128, KCV, 128], BF16)
    cacc = [sb(f"cacc{i}", [128, TH], F32) for i in range(2)]
    lnmu = sb("lnmu", [128, TH], F32)
    lnrs = sb("lnrs", [128, TH], F32)
    nrstd = sb("nrstd", [128, TH], F32)
    sig2 = [sb(f"sig{i}", [128, TH], F32) for i in range(2)]
    sig = sig2[0]
    t1b = sig2[1]
    NPT = 4
    ptb = [sb(f"pt{i}", [128, NT], BF16) for i in range(NPT)]
    sm = [sb(f"sm{i}", [128, 8], F32) for i in range(2)]
    kst = [sb(f"kst{i}", [128, NT], BF16) for i in range(2)]
    vst = [sb(f"vst{i}", [128, 256], BF16) for i in range(2)]
    halo = sb("halo", [128, 8, 30], BF16)
    ident = sb("ident", [128, 128], F32)
    identb = sb("identb", [128, 128], BF16)
    maskT = sb("maskT", [128, 128], BF16)
    ones = sb("ones", [128, 128], F32)
    epsn = sb("epsn", [128, 1], F32)
    epsl = sb("epsl", [128, 1], F32)
    prms = [sb(f"prm{i}", [128, 128], F32) for i in range(2)]
    sqbb = [sb(f"sqbb{i}", [128, NT], BF16) for i in range(2)]
    onesb = sb("onesb", [128, 128], BF16)
    g1c = sb("g1c", [128, 64], F32)
    g2c = sb("g2c", [128, 64], F32)
    gfc = sb("gfc", [128, 16], F32)
    cbc = sb("cbc", [128, 32], F32)
    lgc = sb("lgc", [128, 32], F32)
    lbc = sb("lbc", [128, 32], F32)
    sgc = sb("sgc", [128, 4], F32)
    cwc = sb("cwc", [128, DEPTH * 8 * KCV], F32)
    lamrow = xbuf[0]
    lamb = sig2[0]
    osb = [sb(f"osb{c}", [128, 4, 132], F32) for c in range(2)]
    o4 = sb("o4", [128, 4, 128], F32)
    sq4 = sb("sq4", [128, 4, 128], F32)
    on4 = sb("on4", [128, 4, 128], F32)
    lame = sb("lame", [128, 16], F32)
    neglam = sb("neglam", [128, 4], F32)

    ps = [nc.alloc_psum_tensor(f"ps{i}", [128, NT], F32) for i in range(8)]

    def bank(i):
        return ("bank", i)

    P.dma("sp", "misc", 2, lambda e: e.dma_start(out=ident[:, :], in_=ident_d.ap()), writes=["ident"])
    P.dma("sp", "misc", 2, lambda e: e.dma_start(out=maskT[:, :], in_=mask_d.ap()), writes=["maskT"])
    P.op("dve", lambda e: e.memset(ones[:, :], 1.0), writes=["ones"])
    P.op("dve", lambda e: e.memset(onesb[:, :], 1.0), writes=["onesb"])
    P.op("dve", lambda e: e.memset(epsn[:, :], EPS), writes=["epsn"])
    P.op("dve", lambda e: e.memset(epsl[:, :], LN_EPS), writes=["epsl"])
    P.op("dve", lambda e: e.tensor_copy(out=identb[:, :], in_=ident[:, :]), reads=["ident"], writes=["identb"])
    for i in range(2):
        P.op("dve", lambda e, i=i: e.memset(vh[i][:, :, 128:129], 1.0), writes=[("vh1", i)])

    trs = {"n": 0}

    def tr_load(src_ap, r, dst_ap, tag):
        i = trs["n"]
        trs["n"] += 1
        prm = prms[i % 2]
        pb = 4 + i % 4
        P.dma("sp", "misc", 4, lambda e: e.dma_start(out=prm[0:r, :], in_=src_ap), writes=[("prm", i % 2)])
        P.op("pe", lambda e: e.transpose(ps[pb][:, 0:r], prm[0:r, :], ident[0:r, 0:r]),
             reads=[("prm", i % 2), "ident"], writes=[bank(pb)])
        P.op("dve", lambda e: e.tensor_copy(out=dst_ap, in_=ps[pb][:, 0:r]), reads=[bank(pb)], writes=[tag])

    tr_load(norm1_g.ap().rearrange("l (c p) -> (l c) p", p=128), 64, g1c[:, :], "g1c")
    tr_load(norm2_g.ap().rearrange("l (c p) -> (l c) p", p=128), 64, g2c[:, :], "g2c")
    tr_load(final_g.ap().rearrange("(c p) -> c p", p=128), 16, gfc[:, :], "gfc")
    tr_load(conv_b.ap().rearrange("l (c p) -> (l c) p", p=128), 32, cbc[:, :], "cbc")
    tr_load(conv_ln_g.ap().rearrange("l (c p) -> (l c) p", p=128), 32, lgc[:, :], "lgc")
    tr_load(conv_ln_b.ap().rearrange("l (c p) -> (l c) p", p=128), 32, lbc[:, :], "lbc")
    tr_load(subln_g.ap(), 4, sgc[:, :], "sgc")
    for l in range(n_layers):
        lam_init = 0.8 - 0.6 * math.exp(-0.3 * l)
        P.op("dve", lambda e, l=l, v=1.0 - lam_init: e.tensor_scalar(
            sgc[:, l:l + 1], sgc[:, l:l + 1], v, None, ALU.mult), reads=["sgc"], writes=["sgc"])
        for c in range(8):
            o = (l * 8 + c) * KCV
            tr_load(conv_w.ap()[l, :, c * 128:(c + 1) * 128], KCV, cwc[:, o:o + KCV], "cwc")
    for j in range(4):
        P.dma("sp", "misc", 2, lambda e, j=j: e.dma_start(
            out=lamrow[0:1, j * 256:(j + 1) * 256],
            in_=lam_d[j].ap().rearrange("l d -> (l d)").unsqueeze(0)), writes=["lamrow"])
    for j in range(2):
        P.op("pe", lambda e, j=j: e.matmul(ps[6][:, :], lhsT=ones[0:1, :], rhs=lamrow[0:1, j * 512:(j + 1) * 512],
                                           start=True, stop=True),
             reads=["ones", "lamrow"], writes=[bank(6)])
        P.op("dve", lambda e, j=j: e.tensor_copy(out=lamb[:, j * 512:(j + 1) * 512], in_=ps[6][:, :]),
             reads=[bank(6)], writes=["lamb"])
    for j in range(2):
        P.op("dve", lambda e, j=j: e.tensor_tensor(
            out=lamb[:, j * 512:j * 512 + 256], in0=lamb[:, j * 512:j * 512 + 256],
            in1=lamb[:, j * 512 + 256:j * 512 + 512], op=ALU.mult), reads=["lamb"], writes=["lamb"])
        P.op("dve", lambda e, j=j: e.reduce_sum(
            out=lame[:, j * 4:(j + 1) * 4],
            in_=lamb[:, j * 512:j * 512 + 256].rearrange("p (l d) -> p l d", d=64), axis=AX.X),
            reads=["lamb"], writes=["lame"])
    P.op("act", lambda e: e.activation(out=lame[:, 8:16], in_=lame[:, 0:8], func=AF.Exp),
         reads=["lame"], writes=["lame"])
    P.op("dve", lambda e: e.tensor_tensor(out=neglam[:, :], in0=lame[:, 12:16], in1=lame[:, 8:12], op=ALU.subtract),
         reads=["lame"], writes=["neglam"])
    for l in range(n_layers):
        lam_init = 0.8 - 0.6 * math.exp(-0.3 * l)
        P.op("dve", lambda e, l=l, v=-lam_init: e.tensor_scalar(
            neglam[:, l:l + 1], neglam[:, l:l + 1], v, None, ALU.add), reads=["neglam"], writes=["neglam"])

    xrow = [big32[:, 0:2048], big32[:, 2048:4096]]
    stg = [big32[:, 4096:6144], big32[:, 6144:8192]]
    for tb in range(16):
        sl = tb % 2
        P.dma("pool", "xrl", 2, lambda e, tb=tb, sl=sl: e.dma_start(
            out=xrow[sl], in_=x_in.ap()[tb * 128:(tb + 1) * 128, :]), writes=[("xrow", sl)])
        for g4 in range(4):
            b = (tb * 4 + g4) % 4
            for j in range(4):
                kc = g4 * 4 + j
                P.op("pe", lambda e, b=b, j=j, kc=kc, sl=sl: e.transpose(
                    ps[b][:, j * 128:(j + 1) * 128], xrow[sl][:, kc * 128:(kc + 1) * 128], ident[:, :]),
                    reads=[("xrow", sl), "ident"], writes=[bank(b)], signal=(j == 3))
            eng = "act" if g4 % 2 else "dve"
            if eng == "act":
                P.op("act", lambda e, b=b, g4=g4, sl=sl: e.copy(out=stg[sl][:, g4 * 512:(g4 + 1) * 512], in_=ps[b][:, :]),
                     reads=[bank(b)], writes=[("stg", sl, g4)])
            else:
                P.op("dve", lambda e, b=b, g4=g4, sl=sl: e.tensor_copy(out=stg[sl][:, g4 * 512:(g4 + 1) * 512], in_=ps[b][:, :]),
                     reads=[bank(b)], writes=[("stg", sl, g4)])
        P.dma("sp", "xs", 3, lambda e, tb=tb, sl=sl: e.dma_start(
            out=xT_d[:, :, tb * 128:(tb + 1) * 128], in_=stg[sl].rearrange("p (k t) -> p k t", t=128)),
            reads=[("stg", sl, g) for g in range(4)], writes=[("xtb", tb)])

    if debug:
        for kc in range(16):
            tk = P.dma("sp", "dbg", 2, lambda e, kc=kc: e.dma_start(out=dbg_x0.ap()[:, kc, :], in_=xT_d[:, kc, :]),
                  reads=[("xtb", tb) for tb in range(16)], writes=[("dbgx0", kc)])
        P.wait_all("sp", [tk])
    wstate = {"n": 0}

    def wload(src3, ncols_total):
        i = wstate["n"]
        wstate["n"] += 1
        sl = i % NW
        kcn = src3.shape[1]
        ncol = src3.shape[2]
        dst = wb[sl][:, 0:kcn * ncol].rearrange("p (k n) -> p k n", n=ncol)
        P.dma("pool", "w", NW, lambda e: e.dma_start(out=dst, in_=src3), writes=[("wb", sl)])
        return sl, dst

    def xkeys(s, kc):
        return [("x", kc, s)] + [("xtb", tb) for tb in range(16)]

    xb_state = {"n": 0}

    def xload(s, kc):
        i = xb_state["n"]
        xb_state["n"] += 1
        sl = i % NXB
        P.dma("sp", "xl", 3, lambda e: e.dma_start(out=xbuf[sl][:, :], in_=xT_d[:, kc, s * TH:(s + 1) * TH]),
              reads=xkeys(s, kc), writes=[("xb", sl)])
        return sl

    def norm_stats(s):
        pend = [xload(s, 0), xload(s, 1)]
        for kc in range(16):
            sl = pend.pop(0)
            if kc + 2 < 16:
                pend.append(xload(s, kc + 2))
            for tt in range(2):
                q = (kc * 2 + tt) % 2
                P.op("act", lambda e, sl=sl, tt=tt, q=q: e.activation(
                    out=sqbb[q][:, :], in_=xbuf[sl][:, tt * NT:(tt + 1) * NT], func=AF.Square),
                    reads=[("xb", sl)], writes=[("sqbb", q)])
                P.op("pe", lambda e, tt=tt, q=q, kc=kc: e.matmul(
                    ps[4 + tt][:, :], lhsT=onesb[:, :], rhs=sqbb[q][:, :], start=(kc == 0), stop=(kc == 15)),
                    reads=["onesb", ("sqbb", q)], writes=[bank(4 + tt)], signal=True)
        for tt in range(2):
            P.op("act", lambda e, tt=tt: e.activation(
                out=nrstd[:, tt * NT:(tt + 1) * NT], in_=ps[4 + tt][:, :], func=AF.Sqrt,
                bias=epsn[:, 0:1], scale=1.0 / D), reads=[bank(4 + tt), "epsn"], writes=[("nrstd", tt)])
            P.op("dve", lambda e, tt=tt: e.reciprocal(
                out=nrstd[:, tt * NT:(tt + 1) * NT], in_=nrstd[:, tt * NT:(tt + 1) * NT]),
                reads=[("nrstd", tt)], writes=[("nrstd", tt)])

    def norm_apply_h(s, gcols):
        pend = [xload(s, 0), xload(s, 1)]
        for kc in range(16):
            sl = pend.pop(0)
            if kc + 2 < 16:
                pend.append(xload(s, kc + 2))
            P.op("dve", lambda e, sl=sl, kc=kc: e.scalar_tensor_tensor(
                out=hm[:, kc, :], in0=xbuf[sl][:, :], scalar=gcols[:, kc:kc + 1], in1=nrstd[:, :],
                op0=ALU.mult, op1=ALU.mult),
                reads=[("xb", sl), ("nrstd", 0), ("nrstd", 1), "g1c", "g2c"], writes=[("hm", kc)])

    def wsrc(wt, l, kc0, kcn, c0, ncol):
        return wt.ap()[l].rearrange("(k p) n -> p k n", p=128)[:, kc0:kc0 + kcn, c0:c0 + ncol]

    for l in range(n_layers):
        P.epoch = l
        lam_init = 0.8 - 0.6 * math.exp(-0.3 * l)
        for s in range(2):
            t0 = s * TH
            qT = lambda h: bigb[:, h * TH:(h + 1) * TH]
            qAB = lambda c, h: bigb[:, (c * 8 + h) * TH:(c * 8 + h + 1) * TH]
            qkeys = [("q", h, tt) for h in range(NH) for tt in range(2)] + [("act", fc) for fc in range(16)]
            if l == 0 and s == 0:
                qkeys = qkeys + [("xrow", 0), ("xrow", 1)] + [("stg", sl_, g_) for sl_ in range(2) for g_ in range(4)]
            P.op("pool", lambda e: e.memset(bigb[64:128, 0:8 * TH], 0.0), writes=qkeys)
            P.op("pool", lambda e: e.memset(bigb[0:64, 8 * TH:16 * TH], 0.0), writes=qkeys)
            actT = lambda fc: bigb[:, fc * TH:(fc + 1) * TH]
            norm_stats(s)
            norm_apply_h(s, g1c[:, l * 16:(l + 1) * 16])

            tiles = []
            for j in range(4):
                tiles.append(("k", j, 3072 + j * 256))
            for j in range(4):
                tiles.append(("v", j, 4096 + j * 256))
            for j in range(4):
                tiles.append(("q", j, 2048 + j * 256))
            for j in range(4):
                tiles.append(("g", j, 1024 + j * 256))
                tiles.append(("a", j, j * 256))
            loads = []

            def ensure_loads(upto, tiles=tiles, loads=loads, l=l):
                while len(loads) < min(upto, len(tiles)):
                    kind, j, c0 = tiles[len(loads)]
                    loads.append(wload(wsrc(w_in, l, 0, 16, c0, 256), 256))

            ensure_loads(3)
            gst = {"gi": 0}
            pending_conv = []
            if s == 0:
                for i in range(2):
                    P.op("dve", lambda e, i=i: e.memset(glu[i][:, 0:30], 0.0), writes=[("glu", i)])
            for ti, (kind, j, c0) in enumerate(tiles):
                ensure_loads(ti + 3)
                wsl, wv = loads[ti]
                if kind == "v":
                    for tb in range(8):
                        b = gst["gi"] % 4
                        gst["gi"] += 1
                        for kc in range(16):
                            P.op("pe", lambda e, b=b, kc=kc, tb=tb, wv=wv: e.matmul(
                                ps[b][:, 0:256], lhsT=hm[:, kc, tb * 128:(tb + 1) * 128], rhs=wv[:, kc, :],
                                start=(kc == 0), stop=(kc == 15)),
                                reads=[("hm", kc), ("wb", wsl)], writes=[bank(b)], signal=(kc == 15))
                        q = tb % 2
                        P.op("act", lambda e, b=b, q=q: e.copy(out=vst[q][:, :], in_=ps[b][:, 0:256]),
                             reads=[bank(b)], writes=[("vst", q)])
                        kb = s * 8 + tb
                        P.dma("sp", "kv", 4, lambda e, q=q, kb=kb, j=j: e.dma_start(
                            out=V_d.ap()[2 * j:2 * j + 2, :, kb, :].rearrange("h p d -> p h d"),
                            in_=vst[q][:, :].rearrange("p (h d) -> p h d", d=128)),
                            reads=[("vst", q)], writes=[("V", 2 * j, kb), ("V", 2 * j + 1, kb)])
                    continue
                for oc in range(2):
                    b0 = (gst["gi"] % 2) * 2
                    gst["gi"] += 1
                    for kc in range(16):
                        for tt in range(2):
                            P.op("pe", lambda e, b=b0 + tt, kc=kc, tt=tt, oc=oc, wv=wv: e.matmul(
                                ps[b][:, :], lhsT=wv[:, kc, oc * 128:(oc + 1) * 128],
                                rhs=hm[:, kc, tt * NT:(tt + 1) * NT], start=(kc == 0), stop=(kc == 15)),
                                reads=[("hm", kc), ("wb", wsl)], writes=[bank(b0 + tt)], signal=(kc == 15))
                    if pending_conv:
                        pending_conv.pop(0)()
                    h = 2 * j + oc
                    for tt in range(2):
                        b = b0 + tt
                        if kind == "k":
                            q = (h * 2 + tt) % 2
                            P.op("act", lambda e, b=b, q=q: e.copy(out=kst[q][:, :], in_=ps[b][:, :]),
                                 reads=[bank(b)], writes=[("kst", q)])
                            P.dma("sp", "kv", 4, lambda e, q=q, h=h, tt=tt: e.dma_start(
                                out=kT_d[h, :, t0 + tt * NT:t0 + (tt + 1) * NT], in_=kst[q][:, :]),
                                reads=[("kst", q)], writes=[("K", h, s * 2 + tt)])
                        elif kind == "q":
                            P.op("dve", lambda e, b=b, h=h, tt=tt: e.tensor_copy(
                                out=qAB(0, h)[0:64, tt * NT:(tt + 1) * NT], in_=ps[b][0:64, :]),
                                reads=[bank(b)], writes=[("q", h, tt)])
                            P.op("dve", lambda e, b=b, h=h, tt=tt: e.tensor_copy(
                                out=qAB(1, h)[64:128, tt * NT:(tt + 1) * NT], in_=ps[b][64:128, :]),
                                reads=[bank(b)], writes=[("q", h, tt)])
                        elif kind == "g":
                            P.op("act", lambda e, b=b, tt=tt, oc=oc: e.activation(
                                out=sig2[oc][:, tt * NT:(tt + 1) * NT], in_=ps[b][:, :], func=AF.Sigmoid),
                                reads=[bank(b)], writes=[("sig", oc, tt)])
                        else:
                            gs = h % 2
                            P.op("dve", lambda e, b=b, tt=tt, gs=gs: e.tensor_tensor(
                                out=glu[gs][:, 30 + tt * NT:30 + (tt + 1) * NT], in0=ps[b][:, :],
                                in1=sig2[gs][:, tt * NT:(tt + 1) * NT], op=ALU.mult),
                                reads=[bank(b), ("sig", gs, tt)], writes=[("glu", gs)])
                    if kind == "a":
                        c = h
                        gs = c % 2
                        if s == 1:
                            P.op("dve", lambda e, gs=gs, c=c: e.tensor_copy(out=glu[gs][:, 0:30], in_=halo[:, c, :]),
                                 reads=[("halo", c)], writes=[("glu", gs)])
                        else:
                            P.op("dve", lambda e, gs=gs, c=c: e.tensor_copy(out=halo[:, c, :], in_=glu[gs][:, TH:TH + 30]),
                                 reads=[("glu", gs)], writes=[("halo", c)])
                        wo = (l * 8 + c) * KCV
                        col = l * 8 + c
                        for k in range(KCV):
                            P.op("dve", lambda e, k=k, wo=wo: e.tensor_scalar(
                                dg[:, k, :], identb[:, :], cwc[:, wo + k:wo + k + 1], None, ALU.mult),
                                reads=["identb", "cwc"], writes=[("dg", k)])

                        def conv_task(c=c, gs=gs, col=col):
                            b0 = (gst["gi"] % 2) * 2
                            gst["gi"] += 1
                            for tt in range(2):
                                for k in range(KCV):
                                    P.op("pe", lambda e, b=b0 + tt, k=k, tt=tt: e.matmul(
                                        ps[b][:, :], lhsT=dg[:, k, :], rhs=glu[gs][:, k + tt * NT:k + (tt + 1) * NT],
                                        start=(k == 0), stop=(k == KCV - 1)),
                                        reads=[("dg", k), ("glu", gs)], writes=[bank(b0 + tt)], signal=(k == KCV - 1))
                            for tt in range(2):
                                q = tt
                                P.op("act", lambda e, b=b0 + tt, tt=tt: e.activation(
                                    out=cacc[gs][:, tt * NT:(tt + 1) * NT], in_=ps[b][:, :], func=AF.Identity,
                                    bias=cbc[:, col:col + 1], scale=1.0),
                                    reads=[bank(b0 + tt), "cbc"], writes=[("cacc", gs)])
                                P.op("act", lambda e, tt=tt, q=q: e.activation(
                                    out=sqbb[q][:, :], in_=cacc[gs][:, tt * NT:(tt + 1) * NT], func=AF.Square),
                                    reads=[("cacc", gs)], writes=[("sqbb", q)])
                                P.op("act", lambda e, tt=tt, q=q: e.copy(
                                    out=cab[q][:, :], in_=cacc[gs][:, tt * NT:(tt + 1) * NT]),
                                    reads=[("cacc", gs)], writes=[("cab", q)])
                                P.op("pe", lambda e, tt=tt, q=q: e.matmul(
                                    ps[4 + tt][:, :], lhsT=onesb[:, :], rhs=cab[q][:, :],
                                    start=(c == 0), stop=(c == 7)),
                                    reads=["onesb", ("cab", q)], writes=[bank(4 + tt)], signal=True)
                                P.op("pe", lambda e, tt=tt, q=q: e.matmul(
                                    ps[6 + tt][:, :], lhsT=onesb[:, :], rhs=sqbb[q][:, :],
                                    start=(c == 0), stop=(c == 7)),
                                    reads=["onesb", ("sqbb", q)], writes=[bank(6 + tt)], signal=True)
                            P.dma("sp", "cs", 2, lambda e: e.dma_start(out=c32_d[:, c, :], in_=cacc[gs][:, :]),
                                  reads=[("cacc", gs)], writes=[("c32", c)])

                        pending_conv.append(conv_task)
            while pending_conv:
                pending_conv.pop(0)()
            for tt in range(2):
                tsl = slice(tt * NT, (tt + 1) * NT)
                P.op("dve", lambda e, tt=tt, tsl=tsl: e.tensor_scalar(
                    lnmu[:, tsl], ps[4 + tt][:, :], 1.0 / CW, None, ALU.mult),
                    reads=[bank(4 + tt)], writes=[("lnmu", tt)])
                P.op("dve", lambda e, tt=tt, tsl=tsl: e.tensor_tensor(
                    out=lnrs[:, tsl], in0=lnmu[:, tsl], in1=lnmu[:, tsl], op=ALU.mult),
                    reads=[("lnmu", tt)], writes=[("lnrs", tt)])
                P.op("dve", lambda e, tt=tt, tsl=tsl: e.scalar_tensor_tensor(
                    out=lnrs[:, tsl], in0=ps[6 + tt][:, :], scalar=1.0 / CW, in1=lnrs[:, tsl],
                    op0=ALU.mult, op1=ALU.subtract),
                    reads=[bank(6 + tt), ("lnrs", tt)], writes=[("lnrs", tt)])
                P.op("act", lambda e, tt=tt, tsl=tsl: e.activation(
                    out=lnrs[:, tsl], in_=lnrs[:, tsl], func=AF.Sqrt, bias=epsl[:, 0:1], scale=1.0),
                    reads=[("lnrs", tt), "epsl"], writes=[("lnrs", tt)])
                P.op("dve", lambda e, tt=tt, tsl=tsl: e.reciprocal(out=lnrs[:, tsl], in_=lnrs[:, tsl]),
                     reads=[("lnrs", tt)], writes=[("lnrs", tt)])
            pend = []

            def cload(c):
                i = xb_state["n"]
                xb_state["n"] += 1
                sl = i % NXB
                P.dma("sp", "xl", 3, lambda e: e.dma_start(out=xbuf[sl][:, :], in_=c32_d[:, c, :]),
                      reads=[("c32", c)], writes=[("xb", sl)])
                return sl

            pend = [cload(0), cload(1)]
            for c in range(8):
                sl = pend.pop(0)
                if c + 2 < 8:
                    pend.append(cload(c + 2))
                P.op("dve", lambda e, sl=sl: e.tensor_tensor(out=t1b[:, :], in0=xbuf[sl][:, :], in1=lnmu[:, :], op=ALU.subtract),
                     reads=[("xb", sl), ("lnmu", 0), ("lnmu", 1)], writes=[("sig", 1, 0), ("sig", 1, 1)])
                P.op("dve", lambda e: e.tensor_tensor(out=t1b[:, :], in0=t1b[:, :], in1=lnrs[:, :], op=ALU.mult),
                     reads=[("sig", 1, 0), ("sig", 1, 1), ("lnrs", 0), ("lnrs", 1)], writes=[("sig", 1, 0), ("sig", 1, 1)])
                col = l * 8 + c
                P.op("act", lambda e, c=c, col=col: e.activation(
                    out=hm[:, c, :], in_=t1b[:, :], func=AF.Silu, bias=lbc[:, col:col + 1], scale=lgc[:, col:col + 1]),
                    reads=[("sig", 1, 0), ("sig", 1, 1), "lbc", "lgc"], writes=[("hm", c)])

            nkb_all = (s + 1) * 8

            def kvload(h, sl):
                P.dma("sp", "kvl", 4, lambda e: e.dma_start(out=kh[sl][:, 0:nkb_all * 128], in_=kT_d[h, :, 0:nkb_all * 128]),
                      reads=[("K", h, i) for i in range((s + 1) * 2)], writes=[("kh", sl)])
                P.dma("sp", "kvl", 4, lambda e: e.dma_start(out=vh[sl][:, 0:nkb_all, 0:128], in_=V_d.ap()[h, :, 0:nkb_all, :]),
                      reads=[("V", h, kb) for kb in range(nkb_all)], writes=[("vh", sl)])

            steps = []
            for h in range(NH):
                for tt in range(2):
                    for c in range(2):
                        for kb in range(s * 8 + tt * 4 + 4):
                            steps.append((h, tt, c, kb))
            ep_pending = []

            def emit_qk(i):
                h, tt, c, kb = steps[i]
                hs = h % 2
                qb0 = s * 8 + tt * 4
                jd = kb - qb0
                n0 = max(jd, 0) * 128
                n = NT - n0
                sb_ = (0, 1, 7)[i % 3]
                pt = i % NPT
                diag = jd >= 0
                P.op("pe", lambda e: e.matmul(
                    ps[sb_][:, 0:n], lhsT=kh[hs][:, kb * 128:(kb + 1) * 128],
                    rhs=qAB(c, h)[:, tt * NT + n0:(tt + 1) * NT], start=True, stop=(not diag)),
                    reads=[("kh", hs), ("q", h, tt)], writes=[bank(sb_)], signal=(not diag))
                if diag:
                    P.op("pe", lambda e: e.matmul(
                        ps[sb_][:, 0:128], lhsT=identb[:, :], rhs=maskT[:, :], start=False, stop=True),
                        reads=["identb", "maskT"], writes=[bank(sb_)], signal=True)
                P.op("act", lambda e: e.activation(
                    out=ptb[pt][:, 0:n], in_=ps[sb_][:, 0:n], func=AF.Exp, scale=0.125),
                    reads=[bank(sb_)], writes=[("pt", pt)])

            def emit_av(i):
                h, tt, c, kb = steps[i]
                hs = h % 2
                qb0 = s * 8 + tt * 4
                n0 = max(kb - qb0, 0) * 128
                pt = i % NPT
                for qi in range(4):
                    if qi * 128 < n0:
                        continue
                    last = (kb == qb0 + qi)
                    P.op("pe", lambda e, qi=qi, last=last: e.matmul(
                        ps[2 + qi][:, 0:129], lhsT=ptb[pt][:, qi * 128 - n0:(qi + 1) * 128 - n0],
                        rhs=vh[hs][:, kb, 0:129], start=(kb == 0), stop=last),
                        reads=[("pt", pt), ("vh", hs), ("vh1", hs)], writes=[bank(2 + qi)], signal=last)
                if kb != qb0 + 3:
                    return
                for qi in range(4):
                    ob = 2 + qi
                    P.op("dve", lambda e, qi=qi, ob=ob: e.tensor_copy(out=osb[c][:, qi, 0:129], in_=ps[ob][:, 0:129]),
                         reads=[bank(ob)], writes=[("osb", c)])
                if c == 0:
                    return
                while ep_pending:
                    ep_pending.pop(0)[1]()
                bc = lambda ap: ap.unsqueeze(2).to_broadcast([128, 4, 128])
                P.op("dve", lambda e: e.reciprocal(out=sm[0][:, 0:4], in_=osb[0][:, :, 128]),
                     reads=[("osb", 0)], writes=["sm0"])
                P.op("dve", lambda e: e.reciprocal(out=sm[1][:, 0:4], in_=osb[1][:, :, 128]),
                     reads=[("osb", 1)], writes=["sm1"])
                P.op("dve", lambda e: e.tensor_scalar(sm[1][:, 0:4], sm[1][:, 0:4], neglam[:, l:l + 1], None, ALU.mult),
                     reads=["sm1", "neglam"], writes=["sm1"])
                P.op("dve", lambda e: e.tensor_tensor(out=osb[0][:, :, 0:128], in0=osb[0][:, :, 0:128],
                                                      in1=bc(sm[0][:, 0:4]), op=ALU.mult),
                     reads=[("osb", 0), "sm0"], writes=[("osb", 0)])
                P.op("dve", lambda e: e.tensor_tensor(out=osb[1][:, :, 0:128], in0=osb[1][:, :, 0:128],
                                                      in1=bc(sm[1][:, 0:4]), op=ALU.mult),
                     reads=[("osb", 1), "sm1"], writes=[("osb", 1)])
                P.op("dve", lambda e: e.tensor_tensor(out=o4[:, :, :], in0=osb[0][:, :, 0:128], in1=osb[1][:, :, 0:128], op=ALU.add),
                     reads=[("osb", 0), ("osb", 1)], writes=["o4"])
                P.op("dve", lambda e: e.tensor_tensor(out=sq4[:, :, :], in0=o4[:, :, :], in1=o4[:, :, :], op=ALU.mult),
                     reads=["o4"], writes=["sq4"])
                P.op("dve", lambda e: e.reduce_sum(out=sm[1][:, 4:8], in_=sq4[:, :, :], axis=AX.X),
                     reads=["sq4"], writes=["ss"])

                def part1c():
                    P.op("act", lambda e: e.activation(out=sm[1][:, 4:8], in_=sm[1][:, 4:8], func=AF.Ln,
                                                       bias=epsn[:, 0:1], scale=1.0 / 128),
                         reads=["ss", "epsn"], writes=["ss"])
                    P.op("act", lambda e: e.activation(out=sm[1][:, 4:8], in_=sm[1][:, 4:8], func=AF.Exp, scale=-0.5),
                         reads=["ss"], writes=["ss"])
                    P.op("dve", lambda e: e.tensor_tensor(out=on4[:, :, :], in0=o4[:, :, :], in1=bc(sm[1][:, 4:8]), op=ALU.mult),
                         reads=["o4", "ss"], writes=["on4"])

                def part2(h=h, tt=tt):
                    for qi in range(4):
                        P.op("pe", lambda e, qi=qi: e.transpose(ps[6][:, qi * 128:(qi + 1) * 128], on4[:, qi, :], ident[:, :]),
                             reads=["on4", "ident"], writes=[bank(6)], signal=(qi == 3))
                    P.op("dve", lambda e: e.tensor_scalar(
                        hm[:, 8 + h, tt * NT:(tt + 1) * NT], ps[6][:, :], sgc[:, l:l + 1], None, ALU.mult),
                        reads=[bank(6), "sgc"], writes=[("hm", 8 + h)])

                ep_pending.append((i + 8, part1c))
                ep_pending.append((i + 12, part2))

            kvload(0, 0)
            LA = 3
            for i0 in range(LA):
                emit_qk(i0)
            for i in range(len(steps)):
                h, tt, c, kb = steps[i]
                if tt == 0 and c == 0 and kb == 0 and h + 1 < NH:
                    kvload(h + 1, (h + 1) % 2)
                emit_av(i)
                while ep_pending and ep_pending[0][0] <= i:
                    ep_pending.pop(0)[1]()
                if i + LA < len(steps):
                    emit_qk(i + LA)
            while ep_pending:
                ep_pending.pop(0)[1]()

            if debug and l == 0:
                out_dbg = []
                out_dbg.append(P.dma("sp", "dbg", 2, lambda e, s=s: e.dma_start(out=dbg_mix.ap()[s], in_=hm[:, :, :]),
                      reads=[("hm", c) for c in range(16)], writes=[("dbgmix", s)]))
                out_dbg.append(P.dma("sp", "dbg", 2, lambda e, s=s: e.dma_start(out=dbg_q.ap()[s], in_=bigb[:, 0:8 * TH]),
                      reads=[("q", h, tt) for h in range(8) for tt in range(2)], writes=[("dbgq", s)]))
                P.wait_all("sp", out_dbg)
            tiles = [(j, j * 256) for j in range(8)]
            loads = []

            def ensure_loads_o(upto, tiles=tiles, loads=loads, l=l):
                while len(loads) < min(upto, len(tiles)):
                    j, c0 = tiles[len(loads)]
                    loads.append(wload(wsrc(w_out, l, 0, 16, c0, 256), 256))

            def resid_gemm(ntiles, ensure, loads, kcn, rhs_fn, rhs_keys, gi0=0):
                gi = gi0
                for ti in range(ntiles):
                    ensure(ti + 3)
                    wsl, wv = loads[ti]
                    noc = wv.shape[2] // 128
                    for oc in range(noc):
                        ocg = ti * noc + oc
                        xsl = xload(s, ocg)
                        b0 = (gi % 2) * 2
                        gi += 1
                        for kc in range(kcn):
                            for tt in range(2):
                                P.op("pe", lambda e, b=b0 + tt, kc=kc, tt=tt, oc=oc, wv=wv: e.matmul(
                                    ps[b][:, :], lhsT=wv[:, kc, oc * 128:(oc + 1) * 128],
                                    rhs=rhs_fn(kc)[:, tt * NT:(tt + 1) * NT], start=(kc == 0), stop=(kc == kcn - 1)),
                                    reads=[rhs_keys(kc), ("wb", wsl)], writes=[bank(b0 + tt)], signal=(kc == kcn - 1))
                        for tt in range(2):
                            P.op("dve", lambda e, b=b0 + tt, tt=tt, xsl=xsl: e.tensor_tensor(
                                out=xbuf[xsl][:, tt * NT:(tt + 1) * NT], in0=xbuf[xsl][:, tt * NT:(tt + 1) * NT],
                                in1=ps[b][:, :], op=ALU.add), reads=[bank(b0 + tt), ("xb", xsl)], writes=[("xb", xsl)])
                        P.dma("sp", "xs", 3, lambda e, xsl=xsl, ocg=ocg: e.dma_start(
                            out=xT_d[:, ocg, s * TH:(s + 1) * TH], in_=xbuf[xsl][:, :]),
                            reads=[("xb", xsl)], writes=[("x", ocg, s)])

            ensure_loads_o(3)
            resid_gemm(8, ensure_loads_o, loads, 16, lambda kc: hm[:, kc, :], lambda kc: ("hm", kc))

            norm_stats(s)
            norm_apply_h(s, g2c[:, l * 16:(l + 1) * 16])

            for fh in range(2):
                f0 = fh * 2816
                tiles = []
                for j in range(11):
                    tiles.append(("g", j))
                    tiles.append(("u", j))
                loads = []

                def ensure_loads_f(upto, tiles=tiles, loads=loads, l=l, f0=f0):
                    while len(loads) < min(upto, len(tiles)):
                        kind, j = tiles[len(loads)]
                        wt = w_gate if kind == "g" else w_up
                        loads.append(wload(wsrc(wt, l, 0, 16, f0 + j * 256, 256), 256))

                ensure_loads_f(3)
                for j in range(11):
                    ensure_loads_f(2 * j + 4)
                    gsl, gv = loads[2 * j]
                    usl, uv = loads[2 * j + 1]
                    for oc in range(2):
                        fc = 2 * j + oc
                        bb = (fc % 2) * 4
                        for (wsl, wv, bo) in ((gsl, gv, 0), (usl, uv, 2)):
                            for kc in range(16):
                                for tt in range(2):
                                    P.op("pe", lambda e, b=bb + bo + tt, kc=kc, tt=tt, oc=oc, wv=wv: e.matmul(
                                        ps[b][:, :], lhsT=wv[:, kc, oc * 128:(oc + 1) * 128],
                                        rhs=hm[:, kc, tt * NT:(tt + 1) * NT], start=(kc == 0), stop=(kc == 15)),
                                        reads=[("hm", kc), ("wb", wsl)], writes=[bank(bb + bo + tt)], signal=(kc == 15))
                        for tt in range(2):
                            P.op("act", lambda e, b=bb + tt, tt=tt: e.activation(
                                out=sig[:, tt * NT:(tt + 1) * NT], in_=ps[b][:, :], func=AF.Silu),
                                reads=[bank(bb + tt)], writes=[("sig", 0, tt)])
                            P.op("dve", lambda e, b=bb + 2 + tt, tt=tt, fc=fc: e.tensor_tensor(
                                out=actT(fc)[:, tt * NT:(tt + 1) * NT], in0=ps[b][:, :],
                                in1=sig[:, tt * NT:(tt + 1) * NT], op=ALU.mult),
                                reads=[bank(bb + 2 + tt), ("sig", 0, tt)], writes=[("act", fc)])
                tiles = list(range(16))
                loads = []

                def ensure_loads_d(upto, tiles=tiles, loads=loads, l=l, fh=fh):
                    while len(loads) < min(upto, len(tiles)):
                        oc = tiles[len(loads)]
                        loads.append(wload(wsrc(w_down, l, fh * 22, 22, oc * 128, 128), 128))

                ensure_loads_d(3)
                resid_gemm(16, ensure_loads_d, loads, 22, lambda kc: actT(kc), lambda kc: ("act", kc))

    P.epoch = n_layers - 1
    rowst = big32[:, 0:8192]
    out_toks = []
    for s in range(2):
        norm_stats(s)
        for tt in range(2):
            pend = []

            def fload(kc, s=s, tt=tt):
                i = xb_state["n"]
                xb_state["n"] += 1
                sl = i % NXB
                P.dma("sp", "xl", 3, lambda e: e.dma_start(
                    out=xbuf[sl][:, 0:NT], in_=xT_d[:, kc, s * TH + tt * NT:s * TH + (tt + 1) * NT]),
                    reads=xkeys(s, kc), writes=[("xb", sl)])
                return sl

            pend = [fload(0), fload(1)]
            for kc in range(16):
                sl = pend.pop(0)
                if kc + 2 < 16:
                    pend.append(fload(kc + 2))
                q = kc % 2
                P.op("dve", lambda e, sl=sl, kc=kc, q=q, tt=tt: e.scalar_tensor_tensor(
                    out=cacc[q][:, 0:NT], in0=xbuf[sl][:, 0:NT], scalar=gfc[:, kc:kc + 1],
                    in1=nrstd[:, tt * NT:(tt + 1) * NT], op0=ALU.mult, op1=ALU.mult),
                    reads=[("xb", sl), ("nrstd", tt), "gfc"], writes=[("cacc", q)])
                b = kc % 4
                for j in range(4):
                    P.op("pe", lambda e, b=b, j=j, q=q: e.transpose(
                        ps[b][:, j * 128:(j + 1) * 128], cacc[q][:, j * 128:(j + 1) * 128], ident[:, :]),
                        reads=[("cacc", q), "ident"], writes=[bank(b)], signal=(j == 3))
                if kc % 2:
                    P.op("act", lambda e, b=b, kc=kc: e.copy(
                        out=rowst.rearrange("p (j f) -> p j f", f=D)[:, :, kc * 128:(kc + 1) * 128],
                        in_=ps[b][:, :].rearrange("p (j f) -> p j f", f=128)),
                        reads=[bank(b)], writes=[("rowst", kc)])
                else:
                    P.op("dve", lambda e, b=b, kc=kc: e.tensor_copy(
                        out=rowst.rearrange("p (j f) -> p j f", f=D)[:, :, kc * 128:(kc + 1) * 128],
                        in_=ps[b][:, :].rearrange("p (j f) -> p j f", f=128)),
                        reads=[bank(b)], writes=[("rowst", kc)])
            tok0 = s * TH + tt * NT
            out_toks.append(P.dma("sp", "ys", 2, lambda e, tok0=tok0: e.dma_start(
                out=y_out.ap()[tok0:tok0 + NT, :].rearrange("(j p) f -> p j f", p=128),
                in_=rowst.rearrange("p (j f) -> p j f", f=D)),
                reads=[("rowst", kc) for kc in range(16)], writes=[("y", s, tt)]))
    P.wait_all("sp", out_toks)

    nc.allow_low_precision("bf16 matmul operands with fp32 PSUM accumulation")
    with ExitStack() as es:
        sems = {}
        for i, key in enumerate(sorted(P.semkeys, key=str)):
            sems[key] = es.enter_context(nc.semaphore(f"s{i}"))
        block = es.enter_context(nc.Block())

        def replay(eng_name):
            def body(e):
                for (fn, waits, inc) in P.ops[eng_name]:
                    for (key, val) in waits:
                        e.wait_ge(sems[key], val)
                    if fn is None:
                        continue
                    ins = fn(e)
                    if inc is not None:
                        ins.then_inc(sems[inc[0]], inc[1])
            return body

        block.tensor(replay("pe"))
        block.scalar(replay("act"))
        block.vector(replay("dve"))
        block.gpsimd(replay("pool"))
        block.sync(replay("sp"))
    return nc


_CACHE = {}


def _consts():
    ident = np.eye(128, dtype=np.float32)
    k = np.arange(128)[:, None]
    q = np.arange(128)[None, :]
    mask = np.where(q >= k, 0.0, -30000.0).astype(np.float32).astype(ml_dtypes.bfloat16)
    return ident, mask


def kernel(**inputs):
    if "nc" not in _CACHE:
        _CACHE["nc"] = build(DEPTH)
    nc = _CACHE["nc"]
    ident, mask = _consts()
    shared = {}
    for k, v in inputs.items():
        if k == "x":
            continue
        shared[k] = np.ascontiguousarray(np.asarray(v, dtype=np.float32))
    shared["c_ident"] = ident
    shared["c_mask"] = mask
    x = np.asarray(inputs["x"], dtype=np.float32)
    in_maps = []
    for b in range(NCORES):
        m = dict(shared)
        m["x"] = np.ascontiguousarray(x[b])
        in_maps.append(m)
    res = run_bass_kernel_spmd(nc, in_maps, core_ids=list(range(NCORES)))
    return np.stack([np.asarray(res.results[b]["y"], dtype=np.float32) for b in range(NCORES)], axis=0)
```
